# Optimizing a Trainium2 kernel written in Bass

```python
import math
import jax
import jax.numpy as jnp
from jax import lax
import numpy as np

D_MODEL = 2048
BATCH = 4
SEQ = 2048
DEPTH = 1
DEC_BATCH = 128
DEC_SEQ = 8
PAST_LEN = 16384
PAGE_SIZE = 128

S5_WIDTH = D_MODEL // 2
S5_GROUP = 16
S5_GROUPS = S5_WIDTH // S5_GROUP
S5_STATE = 64
GLA_HEADS = 4
GLA_DK = D_MODEL // 16
GLA_DV = D_MODEL // 8
GLA_RANK = 16
GLA_TAU = 16.0
GLA_CHUNK = 64
XA_HEADS = 4
XA_HEAD_DIM = D_MODEL // 8
XA_WIDTH = XA_HEADS * XA_HEAD_DIM
MEM_LEN = 256
N_BRANCH = 3
FFN_HIDDEN = ((8 * D_MODEL + 3 * 256 - 1) // (3 * 256)) * 256
IN_SPLITS = (S5_WIDTH, GLA_HEADS * GLA_DK, GLA_HEADS * GLA_DK, GLA_HEADS * GLA_DV,
             GLA_HEADS * GLA_DV, GLA_RANK, XA_WIDTH, N_BRANCH * D_MODEL)
IN_WIDTH = sum(IN_SPLITS)
RMS_EPS = 1e-6

kernel_name = 'hybrid_s5_gla_memattn_decoder_step'


def _split_points():
    return [int(v) for v in np.cumsum(IN_SPLITS)[:-1]]


def _rmsnorm(x, g):
    x32 = x.astype(jnp.float32)
    y = x32 * lax.rsqrt(jnp.mean(x32 * x32, axis=-1, keepdims=True) + RMS_EPS)
    return (y * g.astype(jnp.float32)).astype(x.dtype)


def _complex_affine_combine(earlier, later):
    a1r, a1i, b1r, b1i = earlier
    a2r, a2i, b2r, b2i = later
    return (a2r * a1r - a2i * a1i,
            a2r * a1i + a2i * a1r,
            a2r * b1r - a2i * b1i + b2r,
            a2r * b1i + a2i * b1r + b2i)


def _s5_branch(u, h0_re, h0_im, lam_re, lam_im, log_dt, b_re, b_im, c_re, c_im, d_skip, w_glu, b_glu):
    f32 = jnp.float32
    bt, length, _ = u.shape
    ug = u.astype(f32).reshape(bt, length, S5_GROUPS, S5_GROUP)
    lam_re = lam_re.astype(f32)
    lam_im = lam_im.astype(f32)
    dt = jnp.exp(log_dt.astype(f32))[:, None]
    mag = jnp.exp(lam_re * dt)
    a_re = mag * jnp.cos(lam_im * dt)
    a_im = mag * jnp.sin(lam_im * dt)
    den = lam_re * lam_re + lam_im * lam_im
    coef_re = ((a_re - 1.0) * lam_re + a_im * lam_im) / den
    coef_im = (a_im * lam_re - (a_re - 1.0) * lam_im) / den
    b_re = b_re.astype(f32)
    b_im = b_im.astype(f32)
    bbar_re = coef_re[..., None] * b_re - coef_im[..., None] * b_im
    bbar_im = coef_re[..., None] * b_im + coef_im[..., None] * b_re
    bu_re = jnp.einsum('blgc,gpc->blgp', ug, bbar_re)
    bu_im = jnp.einsum('blgc,gpc->blgp', ug, bbar_im)
    h0r = h0_re.astype(f32)
    h0i = h0_im.astype(f32)
    bu_re = bu_re.at[:, 0].add(a_re * h0r - a_im * h0i)
    bu_im = bu_im.at[:, 0].add(a_re * h0i + a_im * h0r)
    ar = jnp.broadcast_to(a_re, bu_re.shape)
    ai = jnp.broadcast_to(a_im, bu_re.shape)
    _, _, h_re, h_im = lax.associative_scan(_complex_affine_combine, (ar, ai, bu_re, bu_im), axis=1)
    y = (jnp.einsum('blgp,gcp->blgc', h_re, c_re.astype(f32))
         - jnp.einsum('blgp,gcp->blgc', h_im, c_im.astype(f32))
         + d_skip.astype(f32) * ug)
    y = jax.nn.gelu(y.reshape(bt, length, S5_WIDTH))
    y = y * jax.nn.sigmoid(y @ w_glu.astype(f32) + b_glu.astype(f32))
    return y.astype(u.dtype), h_re[:, -1].astype(h0_re.dtype), h_im[:, -1].astype(h0_im.dtype)


def _gla_chunked(q, k, v, log_a, s0):
    bt, length, heads, dk = q.shape
    dv = v.shape[-1]
    c = math.gcd(length, GLA_CHUNK)
    n = length // c

    def to_chunks(t):
        return t.reshape(bt, n, c, heads, t.shape[-1]).transpose(1, 0, 3, 2, 4)

    mask = jnp.tril(jnp.ones((c, c), dtype=bool))[:, :, None]

    def step(s, xs):
        qc, kc, vc, ac = xs
        b = jnp.cumsum(ac, axis=2)
        o_inter = jnp.einsum('bhid,bhde->bhie', qc * jnp.exp(b), s)
        diff = b[:, :, :, None, :] - b[:, :, None, :, :]
        decay = jnp.exp(jnp.where(mask, diff, -jnp.inf))
        att = jnp.einsum('bhid,bhjd,bhijd->bhij', qc, kc, decay)
        o = o_inter + jnp.einsum('bhij,bhje->bhie', att, vc)
        b_last = b[:, :, -1:, :]
        s_new = (jnp.exp(b_last[:, :, 0, :])[..., None] * s
                 + jnp.einsum('bhjd,bhje->bhde', kc * jnp.exp(b_last - b), vc))
        return s_new, o

    s_fin, o = lax.scan(step, s0, (to_chunks(q), to_chunks(k), to_chunks(v), to_chunks(log_a)))
    o = o.transpose(1, 0, 3, 2, 4).reshape(bt, length, heads, dv)
    return o, s_fin


def _gla_branch(q, k, v, r, a_low, s0, w_a2, b_a, g_norm):
    f32 = jnp.float32
    bt, length, _ = q.shape
    qh = q.astype(f32).reshape(bt, length, GLA_HEADS, GLA_DK) * (GLA_DK ** -0.5)
    kh = k.astype(f32).reshape(bt, length, GLA_HEADS, GLA_DK)
    vh = v.astype(f32).reshape(bt, length, GLA_HEADS, GLA_DV)
    log_a = jax.nn.log_sigmoid(a_low.astype(f32) @ w_a2.astype(f32) + b_a.astype(f32)) / GLA_TAU
    log_a = log_a.reshape(bt, length, GLA_HEADS, GLA_DK)
    o, s_fin = _gla_chunked(qh, kh, vh, log_a, s0.astype(f32))
    o = o * lax.rsqrt(jnp.mean(o * o, axis=-1, keepdims=True) + RMS_EPS)
    o = o.reshape(bt, length, GLA_HEADS * GLA_DV) * g_norm.astype(f32)
    o = o * jax.nn.silu(r.astype(f32))
    return o.astype(q.dtype), s_fin.astype(s0.dtype)


def _memory_attention(q, mem_k, mem_v):
    bt, length, _ = q.shape
    qh = q.astype(jnp.float32).reshape(bt, length, XA_HEADS, XA_HEAD_DIM)
    scores = jnp.einsum('blhd,bmhd->bhlm', qh, mem_k.astype(jnp.float32)) * (XA_HEAD_DIM ** -0.5)
    p = jax.nn.softmax(scores, axis=-1)
    o = jnp.einsum('bhlm,bmhd->blhd', p, mem_v.astype(jnp.float32)).reshape(bt, length, XA_WIDTH)
    return o.astype(q.dtype)


def _layer(x, mem_k, mem_v, s5_re0, s5_im0, gla_s0, p):
    bt, length, _ = x.shape
    h = _rmsnorm(x, p['norm_mix'])
    z = h @ p['w_in']
    u_s5, q_g, k_g, v_g, r_g, a_low, q_x, gate_logits = jnp.split(z, _split_points(), axis=-1)
    y_s5, s5_re, s5_im = _s5_branch(u_s5, s5_re0, s5_im0, p['s5_lam_re'], p['s5_lam_im'], p['s5_log_dt'],
                                    p['s5_b_re'], p['s5_b_im'], p['s5_c_re'], p['s5_c_im'], p['s5_d'],
                                    p['s5_w_glu'], p['s5_b_glu'])
    y_gla, gla_s = _gla_branch(q_g, k_g, v_g, r_g, a_low, gla_s0, p['gla_w_a2'], p['gla_b_a'], p['gla_norm'])
    y_x = _memory_attention(q_x, mem_k, mem_v)
    gates = jax.nn.sigmoid(gate_logits.astype(jnp.float32)).reshape(bt, length, N_BRANCH, D_MODEL)
    merged = (gates[:, :, 0] * (y_s5 @ p['w_br_s5'])
              + gates[:, :, 1] * (y_gla @ p['w_br_gla'])
              + gates[:, :, 2] * (y_x @ p['w_br_xattn'])).astype(x.dtype)
    x = x + merged @ p['w_out']
    hf = _rmsnorm(x, p['norm_ffn'])
    x = x + (jax.nn.silu(hf @ p['w_ffn_gate']) * (hf @ p['w_ffn_up'])) @ p['w_ffn_down']
    return x, s5_re, s5_im, gla_s


def setup_inputs(seed: int = 0) -> dict:
    key = jax.random.key(seed)
    ks = jax.random.split(key, 36)
    f32 = jnp.float32
    L = DEPTH
    G = S5_GROUPS
    P = S5_STATE

    def nrm(k, shape, scale):
        return jax.random.normal(k, shape, f32) * scale

    def gain(k, shape):
        return 1.0 + 0.01 * jax.random.normal(k, shape, f32)

    x_prompt = nrm(ks[0], (BATCH, SEQ, D_MODEL), 1.0)
    x_sample = nrm(ks[1], (DEC_BATCH, DEC_SEQ, D_MODEL), 1.0)
    mem_prompt = nrm(ks[2], (BATCH, MEM_LEN, D_MODEL), 1.0)
    state_s5_re = nrm(ks[3], (L, DEC_BATCH, G, P), 0.1)
    state_s5_im = nrm(ks[4], (L, DEC_BATCH, G, P), 0.1)
    state_gla = nrm(ks[5], (L, DEC_BATCH, GLA_HEADS, GLA_DK, GLA_DV), 0.5)
    cache_mem_k = nrm(ks[6], (L, DEC_BATCH, MEM_LEN, XA_HEADS, XA_HEAD_DIM), 1.0)
    cache_mem_v = nrm(ks[7], (L, DEC_BATCH, MEM_LEN, XA_HEADS, XA_HEAD_DIM), 1.0)
    norm_mix = gain(ks[8], (L, D_MODEL))
    w_in = nrm(ks[9], (L, D_MODEL, IN_WIDTH), D_MODEL ** -0.5)
    s5_lam_re = -0.5 + nrm(ks[10], (L, G, P), 0.01)
    s5_lam_im = math.pi * jnp.arange(P, dtype=f32) + nrm(ks[11], (L, G, P), 0.01)
    s5_log_dt = jax.random.uniform(ks[12], (L, G), f32, math.log(1e-3), math.log(1e-1))
    s5_b_re = nrm(ks[13], (L, G, P, S5_GROUP), (2 * S5_GROUP) ** -0.5)
    s5_b_im = nrm(ks[14], (L, G, P, S5_GROUP), (2 * S5_GROUP) ** -0.5)
    s5_c_re = nrm(ks[15], (L, G, S5_GROUP, P), P ** -0.5)
    s5_c_im = nrm(ks[16], (L, G, S5_GROUP, P), P ** -0.5)
    s5_d = nrm(ks[17], (L, G, S5_GROUP), 1.0)
    s5_w_glu = nrm(ks[18], (L, S5_WIDTH, S5_WIDTH), S5_WIDTH ** -0.5)
    s5_b_glu = nrm(ks[19], (L, S5_WIDTH), 0.01)
    gla_w_a2 = nrm(ks[20], (L, GLA_RANK, GLA_HEADS * GLA_DK), GLA_RANK ** -0.5)
    gla_b_a = nrm(ks[21], (L, GLA_HEADS * GLA_DK), 0.1)
    gla_norm = gain(ks[22], (L, GLA_HEADS * GLA_DV))
    mem_norm = gain(ks[23], (L, D_MODEL))
    w_mem_k = nrm(ks[24], (L, D_MODEL, XA_WIDTH), D_MODEL ** -0.5)
    w_mem_v = nrm(ks[25], (L, D_MODEL, XA_WIDTH), D_MODEL ** -0.5)
    w_br_s5 = nrm(ks[26], (L, S5_WIDTH, D_MODEL), S5_WIDTH ** -0.5)
    w_br_gla = nrm(ks[27], (L, GLA_HEADS * GLA_DV, D_MODEL), (GLA_HEADS * GLA_DV) ** -0.5)
    w_br_xattn = nrm(ks[28], (L, XA_WIDTH, D_MODEL), XA_WIDTH ** -0.5)
    w_out = nrm(ks[29], (L, D_MODEL, D_MODEL), D_MODEL ** -0.5)
    norm_ffn = gain(ks[30], (L, D_MODEL))
    w_ffn_gate = nrm(ks[31], (L, D_MODEL, FFN_HIDDEN), D_MODEL ** -0.5)
    w_ffn_up = nrm(ks[32], (L, D_MODEL, FFN_HIDDEN), D_MODEL ** -0.5)
    w_ffn_down = nrm(ks[33], (L, FFN_HIDDEN, D_MODEL), FFN_HIDDEN ** -0.5)
    norm_final = gain(ks[34], (D_MODEL,))
    return {'x_prompt': x_prompt, 'x_sample': x_sample, 'mem_prompt': mem_prompt,
            'state_s5_re': state_s5_re, 'state_s5_im': state_s5_im, 'state_gla': state_gla,
            'cache_mem_k': cache_mem_k, 'cache_mem_v': cache_mem_v,
            'norm_mix': norm_mix, 'w_in': w_in,
            's5_lam_re': s5_lam_re, 's5_lam_im': s5_lam_im, 's5_log_dt': s5_log_dt,
            's5_b_re': s5_b_re, 's5_b_im': s5_b_im, 's5_c_re': s5_c_re, 's5_c_im': s5_c_im,
            's5_d': s5_d, 's5_w_glu': s5_w_glu, 's5_b_glu': s5_b_glu,
            'gla_w_a2': gla_w_a2, 'gla_b_a': gla_b_a, 'gla_norm': gla_norm,
            'mem_norm': mem_norm, 'w_mem_k': w_mem_k, 'w_mem_v': w_mem_v,
            'w_br_s5': w_br_s5, 'w_br_gla': w_br_gla, 'w_br_xattn': w_br_xattn, 'w_out': w_out,
            'norm_ffn': norm_ffn, 'w_ffn_gate': w_ffn_gate, 'w_ffn_up': w_ffn_up, 'w_ffn_down': w_ffn_down,
            'norm_final': norm_final}


def reference(x_prompt, x_sample, mem_prompt, state_s5_re, state_s5_im, state_gla, cache_mem_k, cache_mem_v,
              norm_mix, w_in, s5_lam_re, s5_lam_im, s5_log_dt, s5_b_re, s5_b_im, s5_c_re, s5_c_im,
              s5_d, s5_w_glu, s5_b_glu, gla_w_a2, gla_b_a, gla_norm, mem_norm, w_mem_k, w_mem_v,
              w_br_s5, w_br_gla, w_br_xattn, w_out, norm_ffn, w_ffn_gate, w_ffn_up, w_ffn_down, norm_final):
    bp = x_prompt.shape[0]
    xp = x_prompt
    xs = x_sample
    p_s5_re, p_s5_im, p_gla, p_mk, p_mv = [], [], [], [], []
    s_s5_re, s_s5_im, s_gla = [], [], []
    for l in range(DEPTH):
        p = {'norm_mix': norm_mix[l], 'w_in': w_in[l],
             's5_lam_re': s5_lam_re[l], 's5_lam_im': s5_lam_im[l], 's5_log_dt': s5_log_dt[l],
             's5_b_re': s5_b_re[l], 's5_b_im': s5_b_im[l], 's5_c_re': s5_c_re[l], 's5_c_im': s5_c_im[l],
             's5_d': s5_d[l], 's5_w_glu': s5_w_glu[l], 's5_b_glu': s5_b_glu[l],
             'gla_w_a2': gla_w_a2[l], 'gla_b_a': gla_b_a[l], 'gla_norm': gla_norm[l],
             'w_br_s5': w_br_s5[l], 'w_br_gla': w_br_gla[l], 'w_br_xattn': w_br_xattn[l], 'w_out': w_out[l],
             'norm_ffn': norm_ffn[l], 'w_ffn_gate': w_ffn_gate[l], 'w_ffn_up': w_ffn_up[l],
             'w_ffn_down': w_ffn_down[l]}
        mem_h = _rmsnorm(mem_prompt, mem_norm[l])
        mk = (mem_h @ w_mem_k[l]).reshape(bp, mem_prompt.shape[1], XA_HEADS, XA_HEAD_DIM)
        mv = (mem_h @ w_mem_v[l]).reshape(bp, mem_prompt.shape[1], XA_HEADS, XA_HEAD_DIM)
        z_re = jnp.zeros((bp, S5_GROUPS, S5_STATE), dtype=x_prompt.dtype)
        z_im = jnp.zeros((bp, S5_GROUPS, S5_STATE), dtype=x_prompt.dtype)
        z_gla = jnp.zeros((bp, GLA_HEADS, GLA_DK, GLA_DV), dtype=x_prompt.dtype)
        xp, pr, pi, pg = _layer(xp, mk, mv, z_re, z_im, z_gla, p)
        xs, sr, si, sg = _layer(xs, cache_mem_k[l], cache_mem_v[l], state_s5_re[l], state_s5_im[l], state_gla[l], p)
        p_s5_re.append(pr)
        p_s5_im.append(pi)
        p_gla.append(pg)
        p_mk.append(mk)
        p_mv.append(mv)
        s_s5_re.append(sr)
        s_s5_im.append(si)
        s_gla.append(sg)
    y_prompt = _rmsnorm(xp, norm_final)
    y_sample = _rmsnorm(xs, norm_final)
    return (y_prompt, y_sample,
            jnp.stack(p_s5_re), jnp.stack(p_s5_im), jnp.stack(p_gla), jnp.stack(p_mk), jnp.stack(p_mv),
            jnp.stack(s_s5_re), jnp.stack(s_s5_im), jnp.stack(s_gla))
```

```python
import math
import os
from contextlib import ExitStack
import numpy as np
import ml_dtypes
import concourse.bass as bass
import concourse.mybir as mybir
from concourse.bass_utils import run_bass_kernel_spmd

F32 = mybir.dt.float32
F32R = mybir.dt.float32r
BF16 = mybir.dt.bfloat16
AF = mybir.ActivationFunctionType
ALU = mybir.AluOpType

D = 2048
NCH = 16
G = 64
PST = 64
NGP = 32
FFN = 5632
IN_W = 11280
C_S5, C_Q, C_K, C_V, C_R, C_A, C_QX, C_GATE = 0, 1024, 1536, 2048, 3072, 4096, 4112, 5136
NG = 288
TS = 64
EPS = 1e-6
SAME_ENGINE_SYNC = True


class T:
    __slots__ = ("h", "name", "last_w", "readers", "sem", "ndma")

    def __init__(self, h, name):
        self.h = h
        self.name = name
        self.last_w = None
        self.readers = []
        self.sem = None
        self.ndma = 0

    def __getitem__(self, k):
        return self.h[k]


class Op:
    __slots__ = ("eng", "fn", "deps", "sig", "idx", "dma_tile", "dma_cum", "sigidx")

    def __init__(self, eng, fn):
        self.eng = eng
        self.fn = fn
        self.deps = []
        self.sig = False
        self.dma_tile = None
        self.dma_cum = 0
        self.sigidx = 0


class Prog:
    ENGS = ("pe", "act", "dve", "pool", "sp")

    def __init__(self, nc, es):
        self.nc = nc
        self.es = es
        self.q = {e: [] for e in self.ENGS}
        self.tiles = []
        self.nps = 0
        self.psb = []

    def sb(self, name, shape, dt):
        h = self.es.enter_context(self.nc.sbuf_tensor(name, list(shape), dt))
        t = T(h, name)
        self.tiles.append(t)
        return t

    def init_psum(self):
        self.psall = self.es.enter_context(self.nc.psum_tensor("psall", [128, 4096], F32))
        for i in range(8):
            t = T(self.psall[:, i * 512:(i + 1) * 512], "psb%d" % i)
            self.tiles.append(t)
            self.psb.append(t)
        self.ps_list = list(range(8))

    def ps(self):
        t = self.psb[self.ps_list[self.nps % len(self.ps_list)]]
        self.nps += 1
        return t

    def _rec(self, op, reads, writes, extra=()):
        def _unw(lst):
            out = []
            for r in lst:
                if hasattr(r, "ts"):
                    out.extend(r.ts)
                else:
                    out.append(getattr(r, "t", r))
            return out
        reads = _unw(reads)
        writes = _unw(writes)
        pr_ = [r for r in reads if r.name.startswith("psb")]
        if pr_:
            writes = list(writes) + [r for r in pr_ if r not in writes]
            reads = [r for r in reads if not r.name.startswith("psb")]
        deps = list(extra)
        for r in reads:
            if r.last_w is not None:
                deps.append(r.last_w)
        for w in writes:
            if w.last_w is not None:
                deps.append(w.last_w)
            deps.extend(w.readers)
        seen = set()
        for d in deps:
            if id(d) in seen or d is op:
                continue
            seen.add(id(d))
            op.deps.append(d)
            if d.dma_tile is None and not (d.eng == op.eng and (op.eng == "pe" or not SAME_ENGINE_SYNC)):
                d.sig = True
        for r in reads:
            r.readers.append(op)
        for w in writes:
            w.last_w = op
            w.readers = []
        self.q[op.eng].append(op)

    def op(self, eng, fn, reads=(), writes=()):
        o = Op(eng, fn)
        self._rec(o, reads, writes)
        return o

    def dma(self, queue, fn, tile, reads=(), writes=(), extra=()):
        o = Op(queue, fn)
        o.dma_tile = tile
        tile.ndma += 1
        o.dma_cum = tile.ndma
        self._rec(o, reads, writes, extra)
        return o

    def emit(self):
        nc = self.nc
        es = self.es
        esem = {e: es.enter_context(nc.semaphore("sem_" + e)) for e in self.ENGS}
        for t in self.tiles:
            if t.ndma > 0:
                t.sem = es.enter_context(nc.semaphore("d_" + t.name))
        fin = es.enter_context(nc.semaphore("fin"))
        for e in self.ENGS:
            n = 0
            for o in self.q[e]:
                if o.dma_tile is None and o.sig:
                    n += 1
                    o.sigidx = n
        block = es.enter_context(nc.Block())
        finals = {}
        for e in self.ENGS:
            ops = self.q[e]
            finals[e] = max([o.sigidx for o in ops] + [0])

        def run(e, eng):
            seen = {}
            for o in self.q[e]:
                waits = {}
                for d in o.deps:
                    if d.dma_tile is not None:
                        s, v = d.dma_tile.sem, 16 * d.dma_cum
                    else:
                        if d.eng == e and (e == "pe" or not SAME_ENGINE_SYNC):
                            continue
                        s, v = esem[d.eng], d.sigidx
                    k = id(s)
                    if k not in waits or waits[k][1] < v:
                        waits[k] = (s, v)
                for k, (s, v) in waits.items():
                    if seen.get(k, 0) >= v:
                        continue
                    seen[k] = v
                    eng.wait_ge(s, v)
                ins = o.fn(eng)
                if o.dma_tile is not None:
                    ins.then_inc(o.dma_tile.sem, 16)
                elif o.sig:
                    ins.then_inc(esem[e], 1)
            if e == "sp":
                for e2 in self.ENGS:
                    if e2 != "sp" and finals[e2] > 0:
                        eng.wait_ge(esem[e2], finals[e2])
                for t in self.tiles:
                    if t.ndma > 0:
                        eng.wait_ge(t.sem, 16 * t.ndma)

        @block.tensor
        def _(eng):
            run("pe", eng)

        @block.scalar
        def _(eng):
            run("act", eng)

        @block.vector
        def _(eng):
            run("dve", eng)

        @block.gpsimd
        def _(eng):
            run("pool", eng)

        @block.sync
        def _(eng):
            run("sp", eng)


def _consts():
    c = {}
    c["ident"] = np.eye(128, dtype=np.float32)
    c["ones"] = np.ones((128, 128), np.float32)
    idx = np.arange(128)
    for nm, blk in (("p", 64), ("s", 8)):
        same = (idx[:, None] // blk) == (idx[None, :] // blk)
        c["amask_" + nm] = (same & (idx[:, None] <= idx[None, :])).astype(np.float32)
        c["umat_" + nm] = (same & (idx[:, None] > idx[None, :])).astype(np.float32) / 16.0
        c["rmask_" + nm] = np.broadcast_to(((idx % blk) != 0).astype(np.float32)[None, :], (128, 128)).copy()
        nb = 128 // blk
        c["bsel_" + nm] = ((idx[:, None] // blk) == np.arange(nb)[None, :]).astype(np.float32)
    c["cmask"] = np.concatenate([np.broadcast_to(((idx // 32) == g4).astype(np.float32)[None, :], (128, 128))
                                 for g4 in range(4)], axis=1).copy()
    c["rowmask"] = ((idx[:, None] // 32) == np.arange(4)[None, :]).astype(np.float32)
    return c


def build_program():
    nc = bass.Bass("TRN2", target_bir_lowering=False)
    dram = {}

    def din(name, shape, dt=F32):
        dram[name] = nc.dram_tensor(name, list(shape), dt, kind="ExternalInput").ap()
        return dram[name]

    def dout(name, shape):
        dram[name] = nc.dram_tensor(name, list(shape), F32, kind="ExternalOutput").ap()
        return dram[name]

    xp = din("xp", [1024, D]); xpre = din("xpre", [1024, D]); xs = din("xs", [128, D])
    memp = din("memp", [256, D])
    s5re_s = din("s5re_s", [512, 128]); s5im_s = din("s5im_s", [512, 128])
    gla_s = din("gla_s", [16, 4, 128, 256])
    ck = din("ck", [16, 256, 1024]); cv = din("cv", [16, 256, 1024])
    w_in = din("w_in", [D, IN_W]); w_glu = din("w_glu", [1024, 1024])
    w_mk = din("w_mk", [D, 1024]); w_mv = din("w_mv", [D, 1024])
    w_br = [din("w_br%d" % i, [1024, D]) for i in range(3)]
    w_out = din("w_out", [D, D])
    w_fg = din("w_fg", [D, FFN]); w_fu = din("w_fu", [D, FFN]); w_fd = din("w_fd", [FFN, D])
    w_a2 = din("w_a2aug", [17, 512])
    vecs = din("vecs", [128, 96])
    lamre = din("lamre", [128, NGP]); lamim = din("lamim", [128, NGP]); logdt = din("logdt", [128, NGP])
    btc = din("btc", [8, 128, 256])
    ctp = din("ctp", [NGP, 128, 256])
    CONSTS = _consts()
    cn = {k: din("c_" + k, list(v.shape)) for k, v in CONSTS.items()}

    o_yp = dout("o_yp", [1024, D]); o_ys = dout("o_ys", [128, D])
    o_s5p = dout("o_s5p", [2, NGP, 128])
    o_glap = dout("o_glap", [4, 128, 256])
    o_mk = dout("o_mk", [256, 1024]); o_mv = dout("o_mv", [256, 1024])
    o_s5s = dout("o_s5s", [2, 512, 128])
    o_glas = dout("o_glas", [16, 4, 128, 256])

    with ExitStack() as es:
        P = Prog(nc, es)
        P.init_psum()

        def pool_of(name, n, shape, dt):
            tl = [P.sb("%s%d" % (name, i), shape, dt) for i in range(n)]
            cnt = [0]

            def nxt():
                t = tl[cnt[0] % n]
                cnt[0] += 1
                return t
            return nxt

        ident = P.sb("ident", [128, 128], F32)
        ones_r = P.sb("ones_r", [128, 128], F32R)
        cst = {}
        for k, v in CONSTS.items():
            if k in ("ident", "ones"):
                continue
            cst[k] = P.sb("k_" + k, [128, v.shape[1]], F32R if k.startswith("umat") else F32)
        vec = P.sb("vec", [128, 96], F32)
        wa2 = P.sb("wa2", [32, 512], F32R)
        cosT = P.sb("cosT", [128, NGP * TS], F32)
        sinT = P.sb("sinT", [128, NGP * TS], F32)
        amag = P.sb("amag", [128, NGP], F32)
        bbc = P.sb("bbc", [128, 8 * 256], BF16)
        s5c = P.sb("s5c", [128, 2 * NGP], F32)
        s5h0 = P.sb("s5h0", [128, 2 * 128], F32)
        s5hf = P.sb("s5hf", [128, 2 * 128], F32)
        Sst = [P.sb("Sst%d" % h, [128, 256], F32R) for h in range(4)]
        hT = [P.sb("hT%d" % c, [128, NG], F32R) for c in range(NCH)]
        mg = [P.sb("mg%d" % c, [128, NG], F32R) for c in range(NCH)]
        xT = [P.sb("xT%d" % c, [128, NG], F32R) for c in range(NCH)]

        class RV:
            def __init__(self, t):
                self.t = t

            def __getitem__(self, k):
                return self.t.h[k].bitcast(F32R)
        ybr = [RV(xT[c]) for c in range(8)]
        ygT = [RV(xT[8 + c]) for c in range(8)]
        ybr2 = [P.sb("ybr2_%d" % c, [128, NG], F32R) for c in range(8)]
        ybr_t = [xT[c] for c in range(8)]
        ygT_t = [xT[8 + c] for c in range(8)]

        def f32(ap):
            return ap.bitcast(F32)
        rstd = P.sb("rstd", [128, NG], F32)
        NSLOT = 4
        SLW = 2048
        wslA = es.enter_context(nc.sbuf_tensor("wslA", [128, NSLOT * SLW], F32R))
        wsl = [T(wslA[:, i * SLW:(i + 1) * SLW], "wsl%d" % i) for i in range(NSLOT)]
        P.tiles.extend(wsl)
        wcnt = [0]

        class WS:
            def __init__(self, i0, nu):
                self.ts = [wsl[i0 + j] for j in range(nu)]
                self.o = i0 * SLW

            def __getitem__(self, key):
                p, c = key
                return wslA[p, c.start + self.o:c.stop + self.o]
        kvk = P.sb("kvk", [128, 2048], F32)
        mkT = P.sb("mkT", [128, 8 * 256], F32R)
        alow = P.sb("alow", [32, NG], F32R)
        sc = pool_of("sc", 4, [128, NG], F32)
        mgs = pool_of("mgs", 2, [128, NG], F32)
        big2 = [P.sb("big%d" % i, [128, 2 * NG], F32) for i in range(6)]

        class HV:
            def __init__(self, t, half):
                self.t = t
                self.o = half * NG

            def __getitem__(self, k):
                p, c = k
                c0 = (c.start or 0) + self.o
                c1 = (c.stop if c.stop is not None else NG) + self.o
                return self.t.h[p, c0:c1]
        bigc = [0]

        def s5t():
            t = big2[bigc[0] % 6]
            bigc[0] += 1
            return t
        s5u = pool_of("s5u", 2, [128, NG], F32)
        scr = pool_of("scr", 6, [128, NG], F32R)
        scb = pool_of("scb", 3, [128, 2 * NG], BF16)
        s5ub = pool_of("s5ub", 2, [128, NG], BF16)
        bpad = pool_of("bpad", 2, [128, 256], BF16)
        cpad = pool_of("cpad", 2, [128, 256], BF16)
        sm = pool_of("sm", 24, [128, 32], F32)
        cfp = pool_of("cfp", 3, [128, 2 * TS + 64], F32)
        tmw = pool_of("tmw", 10, [128, 256], F32R)
        stg = pool_of("stg", 3, [128, 512], F32)
        g_la0, g_b0, g_eb0, g_enb, g_rm0, g_rs = (HV(big2[0], 0), HV(big2[0], 1), HV(big2[1], 0), HV(big2[1], 1),
                                                  HV(big2[2], 0), HV(big2[2], 1))
        g_la_t, g_b_t, g_eb_t, g_enb_t, g_rm_t, g_rs_t = big2[0], big2[0], big2[1], big2[1], big2[2], big2[2]
        g_qt_t = P.sb("g_qt", [128, NG], F32R); g_kt_t = P.sb("g_kt", [128, NG], F32R)
        g_qt, g_kt = g_qt_t, g_kt_t
        g_po = [HV(big2[3], 0), HV(big2[3], 1)]
        g_po_t = [big2[3], big2[3]]
        ac63 = P.sb("ac63", [128, NGP], F32); as63 = P.sb("as63", [128, NGP], F32)
        print("SBUF bytes remaining per partition:", nc.sbuf_bytes_remaining)

        def mm(out_ap, lhsT, rhs, start, stop, r, w):
            P.op("pe", lambda e: e.matmul(out_ap, lhsT, rhs, start=start, stop=stop), reads=r, writes=w)

        def tr(out_ap, in_ap, r, w):
            k = in_ap.shape[0]
            P.op("pe", lambda e: e.transpose(out_ap, in_ap, ident[0:k, 0:k]), reads=list(r) + [ident], writes=w)

        def act(out_ap, in_ap, func, r, w, bias=None, scale=None):
            kw = {}
            if bias is not None:
                kw["bias"] = bias
            if scale is not None:
                kw["scale"] = scale
            P.op("act", lambda e: e.activation(out_ap, in_ap, func, **kw), reads=r, writes=w)

        def tt(out_ap, a, b, op, r, w, eng="dve"):
            P.op(eng, lambda e: e.tensor_tensor(out_ap, a, b, op), reads=r, writes=w)

        def ts(out_ap, a, s1, s2, op0, op1, r, w, eng="dve"):
            P.op(eng, lambda e: e.tensor_scalar(out_ap, a, s1, s2, op0, op1), reads=r, writes=w)

        def stt(out_ap, a, s_, b, op0, op1, r, w, eng="dve"):
            P.op(eng, lambda e: e.scalar_tensor_tensor(out_ap, a, s_, b, op0, op1), reads=r, writes=w)

        def cp(out_ap, in_ap, r, w, eng="dve"):
            if eng == "act":
                if os.environ.get("KACTCP", "ident") == "ident":
                    P.op("act", lambda e: e.activation(out_ap, in_ap, AF.Identity), reads=r, writes=w)
                else:
                    P.op("act", lambda e: e.copy(out_ap, in_ap), reads=r, writes=w)
            else:
                P.op(eng, lambda e: e.tensor_copy(out_ap, in_ap), reads=r, writes=w)

        def recip(out_ap, in_ap, r, w):
            P.op("dve", lambda e: e.reciprocal(out_ap, in_ap), reads=r, writes=w)

        def scan(out_ap, d0, d1, init, r, w):
            P.op("dve", lambda e: e.tensor_tensor_scan(out_ap, d0, d1, init, ALU.mult, ALU.add), reads=r, writes=w)

        def load(tile, out_ap, in_ap, queue="sp", extra=()):
            return P.dma(queue, lambda e: e.dma_start(out=out_ap, in_=in_ap), tile, writes=[tile], extra=extra)

        def store(tile, out_ap, in_ap, queue="sp"):
            return P.dma(queue, lambda e: e.dma_start(out=out_ap, in_=in_ap), tile, reads=[tile])

        def wload(W, k0, kc, c0, C):
            nu = (kc * C + SLW - 1) // SLW
            assert nu in (1, 2)
            if nu == 2 and wcnt[0] % 2:
                wcnt[0] += 1
            i0 = wcnt[0] % NSLOT
            wcnt[0] += nu
            ws = WS(i0, nu)
            o = ws[:, 0:kc * C].rearrange("p (k c) -> p k c", k=kc)
            i = W[k0 * 128:(k0 + kc) * 128, c0:c0 + C].rearrange("(k p) c -> p k c", p=128)
            P.dma("pool", lambda e: e.dma_start(out=o, in_=i), ws.ts[0], writes=list(ws.ts))
            return ws

        def wv(t, k, C, a, n):
            return t[:, k * C + a:k * C + a + n]

        load(ident, ident[:, :], cn["ident"][:, :])
        load(ones_r, ones_r[:, :], cn["ones"][:, :], queue="pool")
        for k, t in cst.items():
            load(t, t[:, :], cn[k][:, :], queue="pool" if k.startswith("umat") else "sp")
        load(vec, vec[:, :], vecs[:, :])
        load(wa2, wa2[0:17, :], w_a2[:, :], queue="pool")
        G1, G2, GF, GM, GLN, S5D, BGLU = 0, 16, 32, 48, 64, 72, 80

        lre = sm(); lim = sm(); ldt = sm()
        load(lre, lre[:, :], lamre[:, :]); load(lim, lim[:, :], lamim[:, :]); load(ldt, ldt[:, :], logdt[:, :])
        dtt = sm(); tmp = sm(); psi = sm(); p2 = sm(); sn = sm(); cs = sm(); t1 = sm(); t2 = sm()
        act(dtt[:, :], ldt[:, :], AF.Exp, [ldt], [dtt])
        tt(tmp[:, :], lre[:, :], dtt[:, :], ALU.mult, [lre, dtt], [tmp])
        act(amag[:, :], tmp[:, :], AF.Exp, [tmp], [amag])
        stt(psi[:, :], lim[:, :], 1.0 / 64.0, dtt[:, :], ALU.mult, ALU.mult, [lim, dtt], [psi])
        tt(p2[:, :], psi[:, :], psi[:, :], ALU.mult, [psi], [p2])
        ts(sn[:, :], p2[:, :], -1.0 / 42.0, 1.0, ALU.mult, ALU.add, [p2], [sn])
        tt(sn[:, :], sn[:, :], p2[:, :], ALU.mult, [sn, p2], [sn])
        ts(sn[:, :], sn[:, :], -1.0 / 20.0, 1.0, ALU.mult, ALU.add, [sn], [sn])
        tt(sn[:, :], sn[:, :], p2[:, :], ALU.mult, [sn, p2], [sn])
        ts(sn[:, :], sn[:, :], -1.0 / 6.0, 1.0, ALU.mult, ALU.add, [sn], [sn])
        tt(sn[:, :], sn[:, :], psi[:, :], ALU.mult, [sn, psi], [sn])
        ts(cs[:, :], p2[:, :], -1.0 / 56.0, 1.0, ALU.mult, ALU.add, [p2], [cs])
        tt(cs[:, :], cs[:, :], p2[:, :], ALU.mult, [cs, p2], [cs])
        ts(cs[:, :], cs[:, :], -1.0 / 30.0, 1.0, ALU.mult, ALU.add, [cs], [cs])
        tt(cs[:, :], cs[:, :], p2[:, :], ALU.mult, [cs, p2], [cs])
        ts(cs[:, :], cs[:, :], -1.0 / 12.0, 1.0, ALU.mult, ALU.add, [cs], [cs])
        tt(cs[:, :], cs[:, :], p2[:, :], ALU.mult, [cs, p2], [cs])
        ts(cs[:, :], cs[:, :], -0.5, 1.0, ALU.mult, ALU.add, [cs], [cs])
        for _ in range(6):
            tt(t1[:, :], cs[:, :], cs[:, :], ALU.mult, [cs], [t1])
            tt(t2[:, :], sn[:, :], sn[:, :], ALU.mult, [sn], [t2])
            stt(sn[:, :], sn[:, :], 2.0, cs[:, :], ALU.mult, ALU.mult, [sn, cs], [sn])
            tt(cs[:, :], t1[:, :], t2[:, :], ALU.subtract, [t1, t2], [cs])
        cos3 = cosT[:, :].rearrange("p (g j) -> p g j", j=TS)
        sin3 = sinT[:, :].rearrange("p (g j) -> p g j", j=TS)
        cp(cos3[:, :, 0], cs[:, :], [cs], [cosT])
        cp(sin3[:, :, 0], sn[:, :], [sn], [sinT])
        tA = kvk
        tBt = big2[5]
        n_ = 1
        while n_ < TS:
            for gh in range(2):
                gs = slice(gh * 16, (gh + 1) * 16)
                cr = cos3[:, gs, n_ - 1:n_].to_broadcast([128, 16, n_])
                sr = sin3[:, gs, n_ - 1:n_].to_broadcast([128, 16, n_])
                a3 = tA[:, 0:16 * n_].rearrange("p (g j) -> p g j", j=n_)
                b3 = tBt[:, 0:16 * n_].rearrange("p (g j) -> p g j", j=n_)
                tt(a3, cos3[:, gs, 0:n_], cr, ALU.mult, [cosT], [tA])
                tt(b3, sin3[:, gs, 0:n_], sr, ALU.mult, [sinT], [tBt])
                tt(a3, a3, b3, ALU.subtract, [tA, tBt], [tA])
                tt(b3, cos3[:, gs, 0:n_], sr, ALU.mult, [cosT, sinT], [tBt])
                cp(cos3[:, gs, n_:2 * n_], a3, [tA], [cosT])
                tt(a3, sin3[:, gs, 0:n_], cr, ALU.mult, [sinT, cosT], [tA])
                tt(sin3[:, gs, n_:2 * n_], a3, b3, ALU.add, [tA, tBt], [sinT])
            n_ *= 2
        are = sm(); aim = sm(); den = sm(); cre = sm(); cim = sm()
        tt(are[:, :], amag[:, :], cs[:, :], ALU.mult, [amag, cs], [are])
        tt(aim[:, :], amag[:, :], sn[:, :], ALU.mult, [amag, sn], [aim])
        ts(are[:, :], are[:, :], -1.0, None, ALU.add, ALU.bypass, [are], [are])
        tt(den[:, :], lre[:, :], lre[:, :], ALU.mult, [lre], [den])
        tt(tmp[:, :], lim[:, :], lim[:, :], ALU.mult, [lim], [tmp])
        tt(den[:, :], den[:, :], tmp[:, :], ALU.add, [den, tmp], [den])
        recip(den[:, :], den[:, :], [den], [den])
        tt(cre[:, :], are[:, :], lre[:, :], ALU.mult, [are, lre], [cre])
        tt(tmp[:, :], aim[:, :], lim[:, :], ALU.mult, [aim, lim], [tmp])
        tt(cre[:, :], cre[:, :], tmp[:, :], ALU.add, [cre, tmp], [cre])
        tt(cre[:, :], cre[:, :], den[:, :], ALU.mult, [cre, den], [cre])
        tt(cim[:, :], aim[:, :], lre[:, :], ALU.mult, [aim, lre], [cim])
        tt(tmp[:, :], are[:, :], lim[:, :], ALU.mult, [are, lim], [tmp])
        tt(cim[:, :], cim[:, :], tmp[:, :], ALU.subtract, [cim, tmp], [cim])
        tt(cim[:, :], cim[:, :], den[:, :], ALU.mult, [cim, den], [cim])
        cmask = cst["cmask"]
        bb3 = bbc[:, :].rearrange("p (u x) -> p u x", u=8)
        for uc in range(8):
            pb = P.ps()
            for ri, cf_ in enumerate((cre, cim)):
                for g4 in range(4):
                    gp = uc * 4 + g4
                    bc = sc()
                    ts(bc[:, 0:128], cmask[:, g4 * 128:(g4 + 1) * 128], cf_[:, gp:gp + 1], None, ALU.mult, ALU.bypass,
                       [cmask, cf_], [bc])
                    mm(pb[:, ri * 128:(ri + 1) * 128], bc[:, 0:128], ident[:, :], g4 == 0, g4 == 3, [bc, ident], [pb])
            bs = s5t()
            load(bs, bs[:, 0:256], btc[uc, :, :])
            bt = s5t()
            tt(bt[:, 0:128], bs[:, 0:128], pb[:, 0:128], ALU.mult, [bs, pb], [bt])
            tt(bt[:, 128:256], bs[:, 128:256], pb[:, 128:256], ALU.mult, [bs, pb], [bt])
            tt(bb3[:, uc, 0:128], bt[:, 0:128], bt[:, 128:256], ALU.subtract, [bt], [bbc])
            bt2 = s5t()
            tt(bt2[:, 0:128], bs[:, 0:128], pb[:, 128:256], ALU.mult, [bs, pb], [bt2])
            tt(bt2[:, 128:256], bs[:, 128:256], pb[:, 0:128], ALU.mult, [bs, pb], [bt2])
            tt(bb3[:, uc, 128:256], bt2[:, 0:128], bt2[:, 128:256], ALU.add, [bt2], [bbc])
        P.op("dve", lambda e: e.memset(s5c[:, :], 0.0), writes=[s5c])
        for h in range(4):
            ts(Sst[h][:, 0:256], cmask[:, 0:256], 0.0, None, ALU.mult, ALU.bypass, [cmask], [Sst[h]])
        for a_ in range(0, NG, 96):
            ts(alow[0:32, a_:a_ + 96], cmask[0:32, 0:96], 0.0, 1.0, ALU.mult, ALU.add, [cmask], [alow])
        tt(ac63[:, :], amag[:, :], cos3[:, :, TS - 1], ALU.mult, [amag, cosT], [ac63])
        tt(as63[:, :], amag[:, :], sin3[:, :, TS - 1], ALU.mult, [amag, sinT], [as63])

        def norm_to_hT(n, src, gcol, dsth):
            pq = P.ps()
            for c in range(NCH):
                sq = scr()
                act(sq[:, 0:n], f32(src[c][:, 0:n]), AF.Square, [src[c]], [sq])
                mm(pq[:, 0:n], ones_r[:, :], sq[:, 0:n], c == 0, c == NCH - 1, [ones_r, sq], [pq])
            ts(rstd[:, 0:n], pq[:, 0:n], 1.0 / D, EPS, ALU.mult, ALU.add, [pq], [rstd])
            act(rstd[:, 0:n], rstd[:, 0:n], AF.Sqrt, [rstd], [rstd])
            recip(rstd[:, 0:n], rstd[:, 0:n], [rstd], [rstd])
            for c in range(NCH):
                stt(dsth[c][:, 0:n], f32(src[c][:, 0:n]), vec[:, gcol + c:gcol + c + 1], rstd[:, 0:n], ALU.mult, ALU.mult,
                    [src[c], vec, rstd], [dsth[c]])

        def load_x_group(tiles, accumulate=False):
            for (dr, r0, c0, w) in tiles:
                t = kvk
                load(t, t[0:w, :], dr[r0:r0 + w, :])
                for c4 in range(4):
                    pb = P.ps()
                    for j in range(4):
                        c = c4 * 4 + j
                        tr(pb[:, j * 128:j * 128 + w], t[0:w, c * 128:(c + 1) * 128], [t], [pb])
                    for j in range(4):
                        c = c4 * 4 + j
                        if accumulate:
                            tt(xT[c][:, c0:c0 + w], f32(xT[c][:, c0:c0 + w]), pb[:, j * 128:j * 128 + w], ALU.add,
                               [xT[c], pb], [xT[c]])
                        else:
                            cp(xT[c][:, c0:c0 + w], pb[:, j * 128:j * 128 + w], [pb], [xT[c]],
                               eng=os.environ.get("KCPENG", "act") if j % 2 else "dve")

        def proj_fm(W, c0, ncols, n, src, K, consume, srct=None):
            srct = srct or src
            kc = K // 128
            Cw = (min(ncols, SLW // kc) // 128) * 128
            b = 0
            for s0 in range(0, ncols, Cw):
                cw = min(Cw, ncols - s0)
                t = wload(W, 0, kc, c0 + s0, cw)
                for bb in range(cw // 128):
                    pt = P.ps()
                    for k in range(kc):
                        mm(pt[:, 0:n], wv(t, k, cw, bb * 128, 128), src[k][:, 0:n], k == 0, k == kc - 1, [t, srct[k]], [pt])
                    consume(b, pt)
                    b += 1

        def s5_branch(n, nP, s0, ns, state_only, hook=None, drain=None):
            nsub = nP // TS
            wS = ns * 8
            PB = 2 * nP
            P.ps_list = [5, 6, 7]
            py = P.psb[4]
            RE = P.psall[:, 0:1024].rearrange("p (g c) -> p g c", g=2)
            IM = P.psall[:, 1024:2048].rearrange("p (g c) -> p g c", g=2)
            reT = [P.psb[0], P.psb[1]]
            imT = [P.psb[2], P.psb[3]]
            cnt = [0]

            def Pin(x3):
                return x3[:, :, 0:nP].rearrange("p g (s j) -> p g s j", j=TS)

            def Sin(x3):
                return x3[:, :, nP:nP + wS].rearrange("p g (s t) -> p g s t", t=8)

            def Pt(t):
                return t[:, 0:PB].rearrange("p (s g j) -> p g s j", g=2, j=TS)

            def St(t):
                return t[:, PB:PB + 2 * wS].rearrange("p (g s t) -> p g s t", g=2, t=8)

            UB = {}
            CF = {}

            def emit_bu(b, pr_):
                if pr_ == 0:
                    tw = wload(w_in, 0, 16, C_S5 + b * 128, 128)
                    pt = P.ps()
                    for kk in range(16):
                        mm(pt[:, 0:n], wv(tw, kk, 128, 0, 128), hT[kk][:, 0:n], kk == 0, kk == 15, [tw, hT[kk]], [pt])
                    u_ = s5u(); ub_ = s5ub()
                    cp(u_[:, 0:n], pt[:, 0:n], [pt], [u_], eng="act")
                    cp(ub_[:, 0:n], pt[:, 0:n], [pt], [ub_], eng="act")
                    UB[b] = (u_, ub_)
                u_, ub_ = UB[b]
                bps = []
                for i in range(2):
                    bp = bpad()
                    act(bp[:, 0:256], bb3[:, b, :], AF.Identity, [bbc, cst["rowmask"]], [bp],
                        scale=cst["rowmask"][:, pr_ * 2 + i:pr_ * 2 + i + 1])
                    bps.append(bp)
                for i in range(2):
                    mm(reT[i][:, 0:n], bps[i][:, 0:128], ub_[:, 0:n], True, True, [bps[i], ub_], [reT[i]])
                    mm(imT[i][:, 0:n], bps[i][:, 128:256], ub_[:, 0:n], True, True, [bps[i], ub_], [imT[i]])
                gq = b * 4 + pr_ * 2
                cfn = cfp()
                for i in range(2):
                    act(cfn[:, i * TS:(i + 1) * TS], cst["rmask_p"][:, 0:TS], AF.Identity, [cst["rmask_p"], amag], [cfn],
                        scale=amag[:, gq + i:gq + i + 1])
                    if ns:
                        act(cfn[:, 2 * TS + i * wS:2 * TS + (i + 1) * wS], cst["rmask_s"][:, 0:wS], AF.Identity,
                            [cst["rmask_s"], amag], [cfn], scale=amag[:, gq + i:gq + i + 1])
                CF[(b, pr_)] = cfn

            pend = []
            fin2 = []

            def flush_y():
                while fin2:
                    b_, y_, z_ = fin2.pop(0)
                    tt(ygT[b_][:, 0:n], y_[:, 0:n], z_[:, 0:n], ALU.mult, [y_, z_], [ygT_t[b_]])
                while pend:
                    b_, u_ = pend.pop(0)
                    y = sc(); z = sc()
                    stt(y[:, 0:n], u_[:, 0:n], vec[:, S5D + b_:S5D + b_ + 1], py[:, 0:n], ALU.mult, ALU.add,
                        [u_, vec, py], [y])
                    tt(z[:, 0:n], y[:, 0:n], y[:, 0:n], ALU.mult, [y], [z])
                    ts(z[:, 0:n], z[:, 0:n], 0.044715, 1.0, ALU.mult, ALU.add, [z], [z])
                    tt(z[:, 0:n], z[:, 0:n], y[:, 0:n], ALU.mult, [z, y], [z])
                    act(z[:, 0:n], z[:, 0:n], AF.Sigmoid, [z], [z], scale=1.5957691216057308)
                    fin2.append((b_, y, z))

            units = [(b_, p_) for b_ in range(8) for p_ in range(2)]
            emit_bu(*units[0])
            for ui, (b, pr_) in enumerate(units):
                if True:
                    u, ub = UB[b]
                    gp0 = b * 4 + pr_ * 2
                    cqs = []
                    if not state_only:
                        for i in range(2):
                            cq = cpad()
                            load(cq, cq[:, 0:256], ctp[gp0 + i, :, :], queue="pool")
                            cqs.append(cq)
                    k2 = 0 if state_only else cnt[0] % 2
                    cnt[0] += 1
                    ta, tb = big2[k2 * 2], big2[k2 * 2 + 1]
                    gr, gi = big2[4], big2[5]
                    cP = cos3[:, gp0:gp0 + 2, :].unsqueeze(2).to_broadcast([128, 2, nsub, TS])
                    sP = sin3[:, gp0:gp0 + 2, :].unsqueeze(2).to_broadcast([128, 2, nsub, TS])
                    regs = [(Pin, Pt, cP, sP)]
                    if ns:
                        cS = cos3[:, gp0:gp0 + 2, 0:8].unsqueeze(2).to_broadcast([128, 2, ns, 8])
                        sS = sin3[:, gp0:gp0 + 2, 0:8].unsqueeze(2).to_broadcast([128, 2, ns, 8])
                        regs.append((Sin, St, cS, sS))
                    cf = CF.pop((b, pr_))
                    for (Vi, Vt, c_, s_) in regs:
                        tt(Vt(ta), Vi(RE), c_, ALU.mult, reT + [cosT], [ta])
                        tt(Vt(tb), Vi(IM), s_, ALU.mult, imT + [sinT], [tb])
                        tt(Vt(gr), Vt(ta), Vt(tb), ALU.add, [ta, tb], [gr])
                        tt(Vt(ta), Vi(IM), c_, ALU.mult, imT + [cosT], [ta])
                        tt(Vt(tb), Vi(RE), s_, ALU.mult, reT + [sinT], [tb])
                        tt(Vt(gi), Vt(ta), Vt(tb), ALU.subtract, [ta, tb], [gi])
                    if ui + 1 < len(units):
                        emit_bu(*units[ui + 1])
                    if not state_only:
                        flush_y()
                    if hook:
                        hook()
                    t4 = sm()
                    for (gt, coff) in ((gr, 0), (gi, NGP)):
                        first = gt[:, 0:2 * TS].rearrange("p (g j) -> p g j", g=2)[:, :, 0]
                        tt(t4[:, 0:2], amag[:, gp0:gp0 + 2], s5c[:, coff + gp0:coff + gp0 + 2], ALU.mult, [amag, s5c], [t4])
                        tt(first, first, t4[:, 0:2], ALU.add, [gt, t4], [gt])
                    if ns:
                        for (gt, hoff) in ((gr, 0), (gi, 128)):
                            fs = St(gt)[:, :, :, 0]
                            h0v = s5h0[:, hoff:hoff + 128].rearrange("p (s g) -> p g s", g=NGP)[:, gp0:gp0 + 2, 0:ns]
                            tt(fs, fs, h0v, ALU.add, [gt, s5h0], [gt])
                    for s in range(nsub):
                        sl = slice(s * 2 * TS, (s + 1) * 2 * TS)
                        scan(gr[:, sl], cf[:, 0:2 * TS], gr[:, sl], 0.0, [gr, cf], [gr])
                        scan(gi[:, sl], cf[:, 0:2 * TS], gi[:, sl], 0.0, [gi, cf], [gi])
                        if s < nsub - 1:
                            lr = gr[:, sl].rearrange("p (g j) -> p g j", g=2)[:, :, TS - 1]
                            li = gi[:, sl].rearrange("p (g j) -> p g j", g=2)[:, :, TS - 1]
                            sl2 = slice((s + 1) * 2 * TS, (s + 2) * 2 * TS)
                            nr = gr[:, sl2].rearrange("p (g j) -> p g j", g=2)[:, :, 0]
                            ni = gi[:, sl2].rearrange("p (g j) -> p g j", g=2)[:, :, 0]
                            A_c = ac63[:, gp0:gp0 + 2]; A_s = as63[:, gp0:gp0 + 2]
                            t5 = sm(); t6 = sm()
                            tt(t5[:, 0:2], A_c, lr, ALU.mult, [ac63, gr], [t5])
                            tt(t6[:, 0:2], A_s, li, ALU.mult, [as63, gi], [t6])
                            tt(nr, nr, t5[:, 0:2], ALU.add, [gr, t5], [gr])
                            tt(nr, nr, t6[:, 0:2], ALU.subtract, [gr, t6], [gr])
                            t7 = sm(); t8 = sm()
                            tt(t7[:, 0:2], A_s, lr, ALU.mult, [as63, gr], [t7])
                            tt(t8[:, 0:2], A_c, li, ALU.mult, [ac63, gi], [t8])
                            tt(ni, ni, t7[:, 0:2], ALU.add, [gi, t7], [gi])
                            tt(ni, ni, t8[:, 0:2], ALU.add, [gi, t8], [gi])
                    if ns:
                        slS = slice(PB, PB + 2 * wS)
                        scan(gr[:, slS], cf[:, 2 * TS:2 * TS + 2 * wS], gr[:, slS], 0.0, [gr, cf], [gr])
                        scan(gi[:, slS], cf[:, 2 * TS:2 * TS + 2 * wS], gi[:, slS], 0.0, [gi, cf], [gi])
                    lastP = lambda t: t[:, PB - 2 * TS:PB].rearrange("p (g j) -> p g j", g=2)[:, :, TS - 1]
                    if state_only:
                        c63 = cos3[:, gp0:gp0 + 2, TS - 1]; s63 = sin3[:, gp0:gp0 + 2, TS - 1]
                        t5 = sm(); t6 = sm()
                        tt(t5[:, 0:2], c63, lastP(gr), ALU.mult, [cosT, gr], [t5])
                        tt(t6[:, 0:2], s63, lastP(gi), ALU.mult, [sinT, gi], [t6])
                        tt(s5c[:, gp0:gp0 + 2], t5[:, 0:2], t6[:, 0:2], ALU.subtract, [t5, t6], [s5c])
                        t7 = sm(); t8 = sm()
                        tt(t7[:, 0:2], s63, lastP(gr), ALU.mult, [sinT, gr], [t7])
                        tt(t8[:, 0:2], c63, lastP(gi), ALU.mult, [cosT, gi], [t8])
                        tt(s5c[:, NGP + gp0:NGP + gp0 + 2], t7[:, 0:2], t8[:, 0:2], ALU.add, [t7, t8], [s5c])
                        if hook:
                            hook()
                        continue
                    hb = scb(); hib = scb()
                    HB = hb[:, :].rearrange("p (g c) -> p g c", g=2)
                    HIB = hib[:, :].rearrange("p (g c) -> p g c", g=2)
                    for (Vi, Vt, c_, s_) in regs:
                        tt(Vt(ta), Vt(gr), c_, ALU.mult, [gr, cosT], [ta])
                        tt(Vt(tb), Vt(gi), s_, ALU.mult, [gi, sinT], [tb])
                        tt(Vt(ta), Vt(ta), Vt(tb), ALU.subtract, [ta, tb], [ta])
                    for (Vi, Vt, c_, s_) in regs:
                        act(Vi(HB), Vt(ta), AF.Identity, [ta], [hb])
                    cp(s5c[:, gp0:gp0 + 2], lastP(ta), [ta], [s5c])
                    if ns:
                        fr = s5hf[:, 0:128].rearrange("p (s g) -> p g s", g=NGP)[:, gp0:gp0 + 2, 0:ns]
                        cp(fr, St(ta)[:, :, :, 7], [ta], [s5hf])
                    for (Vi, Vt, c_, s_) in regs:
                        tt(Vt(tb), Vt(gr), s_, ALU.mult, [gr, sinT], [tb])
                        tt(Vt(gr), Vt(gi), c_, ALU.mult, [gi, cosT], [gr])
                        tt(Vt(tb), Vt(tb), Vt(gr), ALU.add, [tb, gr], [tb])
                    for (Vi, Vt, c_, s_) in regs:
                        act(Vi(HIB), Vt(tb), AF.Identity, [tb], [hib], scale=-1.0)
                    cp(s5c[:, NGP + gp0:NGP + gp0 + 2], lastP(tb), [tb], [s5c])
                    if ns:
                        fi = s5hf[:, 128:256].rearrange("p (s g) -> p g s", g=NGP)[:, gp0:gp0 + 2, 0:ns]
                        cp(fi, St(tb)[:, :, :, 7], [tb], [s5hf])
                    for i in range(2):
                        g4 = pr_ * 2 + i
                        mm(py[:, 0:n], cqs[i][:, 0:128], hb[:, i * NG:i * NG + n], g4 == 0, False, [cqs[i], hb], [py])
                        mm(py[:, 0:n], cqs[i][:, 128:256], hib[:, i * NG:i * NG + n], False, g4 == 3, [cqs[i], hib], [py])
                    if hook:
                        hook()
                if not state_only and pr_ == 1:
                    pend.append((b, u))
                    if ui == len(units) - 1:
                        flush_y()

            if not state_only:
                flush_y()
                flush_y()
            if drain:
                drain()
            P.ps_list = list(range(8))
            if state_only:
                return

            def got_glu(b, pt):
                sg = sc()
                act(sg[:, 0:n], pt[:, 0:n], AF.Sigmoid, [pt, vec], [sg], bias=vec[:, BGLU + b:BGLU + b + 1])
                tt(ybr[b][:, 0:n], f32(ygT_t[b][:, 0:n]), sg[:, 0:n], ALU.mult, [ygT_t[b], sg], [ybr_t[b]])
            proj_fm(w_glu, 0, 1024, n, ygT, 1024, got_glu, srct=ygT_t)

        def merge_gen(bi, n, first, src=None, srct=None):
            src = src or ybr
            srct = srct or ybr_t
            Cw = 256
            for s0_ in range(0, D, Cw):
                tb = wload(w_br[bi], 0, 8, s0_, Cw)
                for bb in range(Cw // 128):
                    dc = s0_ // 128 + bb
                    tg = wload(w_in, 0, 16, C_GATE + bi * D + dc * 128, 128)
                    pg = P.ps(); pp = P.ps()
                    for k in range(16):
                        mm(pg[:, 0:n], wv(tg, k, 128, 0, 128), hT[k][:, 0:n], k == 0, k == 15, [tg, hT[k]], [pg])
                    for k in range(8):
                        mm(pp[:, 0:n], wv(tb, k, Cw, bb * 128, 128), src[k][:, 0:n], k == 0, k == 7, [tb, srct[k]], [pp])
                    sg = mgs()
                    act(sg[:, 0:n], pg[:, 0:n], AF.Sigmoid, [pg], [sg])
                    yield
                    if first:
                        tt(mg[dc][:, 0:n], sg[:, 0:n], pp[:, 0:n], ALU.mult, [sg, pp], [mg[dc]])
                    else:
                        tt(sg[:, 0:n], sg[:, 0:n], pp[:, 0:n], ALU.mult, [sg, pp], [sg])
                        tt(mg[dc][:, 0:n], f32(mg[dc][:, 0:n]), sg[:, 0:n], ALU.add, [mg[dc], sg], [mg[dc]])

        def merge_branch(bi, n, first):
            for _ in merge_gen(bi, n, first):
                pass

        def gla_gen(n, tiles, s0, state_only):
            if state_only:
                g_la, g_b, g_eb, g_rm = HV(big2[2], 0), HV(big2[2], 1), HV(big2[3], 0), HV(big2[3], 1)
            else:
                g_la, g_b, g_eb, g_rm = g_la0, g_b0, g_eb0, g_rm0
            ta_ = wload(w_in, 0, 16, C_A, 16)
            pa = P.ps()
            for k in range(16):
                mm(pa[0:16, 0:n], wv(ta_, k, 16, 0, 16), hT[k][:, 0:n], k == 0, k == 15, [ta_, hT[k]], [pa])
            cp(alow[0:16, 0:n], pa[0:16, 0:n], [pa], [alow])
            for (kind, c0, w) in tiles:
                cp(g_rm[:, c0:c0 + w], cst["rmask_" + kind][:, 0:w], [cst["rmask_" + kind]], [g_rm])
            for h in range(4):
                tq = wload(w_in, 0, 16, C_Q + h * 128, 128) if not state_only else None
                tk = wload(w_in, 0, 16, C_K + h * 128, 128) if not state_only else None
                pq = P.ps(); pk = P.ps()
                if not state_only:
                    for k in range(16):
                        mm(pq[:, 0:n], wv(tq, k, 128, 0, 128), hT[k][:, 0:n], k == 0, k == 15, [tq, hT[k]], [pq])
                    for k in range(16):
                        mm(pk[:, 0:n], wv(tk, k, 128, 0, 128), hT[k][:, 0:n], k == 0, k == 15, [tk, hT[k]], [pk])
                if not state_only:
                    tk2 = tk
                    tv = wload(w_in, 0, 16, C_V + h * 256, 256)
                px = P.ps()
                mm(px[:, 0:n], wa2[0:17, h * 128:(h + 1) * 128], alow[0:17, 0:n], True, True, [wa2, alow], [px])
                act(g_la[:, 0:n], px[:, 0:n], AF.Exp, [px], [g_la], scale=-1.0)
                act(g_la[:, 0:n], g_la[:, 0:n], AF.Ln, [g_la], [g_la], bias=1.0)
                scan(g_b[:, 0:n], g_rm[:, 0:n], g_la[:, 0:n], 0.0, [g_la, g_rm], [g_b])
                act(g_eb[:, 0:n], g_b[:, 0:n], AF.Exp, [g_b], [g_eb], scale=-1.0 / 16.0)
                if not state_only:
                    act(g_enb[:, 0:n], g_b[:, 0:n], AF.Exp, [g_b], [g_enb], scale=1.0 / 16.0)
                    stt(g_qt[:, 0:n], pq[:, 0:n], 128.0 ** -0.5, g_eb[:, 0:n], ALU.mult, ALU.mult, [pq, g_eb], [g_qt_t])
                    tt(g_kt[:, 0:n], pk[:, 0:n], g_enb[:, 0:n], ALU.mult, [pk, g_enb], [g_kt_t])
                for (kind, c0, w) in tiles:
                    cs_ = slice(c0, c0 + w)
                    if state_only:
                        tk2 = wload(w_in, 0, 16, C_K + h * 128, 128)
                        tv = wload(w_in, 0, 16, C_V + h * 256, 256)
                    blk = 64 if kind == "p" else 8
                    nb = w // blk
                    pv = P.ps(); pkt = P.ps(); pxt = P.ps()
                    for k in range(16):
                        mm(pv[0:w, 0:256], hT[k][:, cs_], wv(tv, k, 256, 0, 256), k == 0, k == 15, [tv, hT[k]], [pv])
                    for k in range(16):
                        mm(pkt[0:w, 0:128], hT[k][:, cs_], wv(tk2, k, 128, 0, 128), k == 0, k == 15, [tk2, hT[k]], [pkt])
                    mm(pxt[0:w, 0:128], alow[0:17, cs_], wa2[0:17, h * 128:(h + 1) * 128], True, True, [alow, wa2], [pxt])
                    vt = tmw()
                    cp(vt[0:w, 0:256], pv[0:w, 0:256], [pv], [vt], eng="act")
                    lat = scr()
                    act(lat[0:w, 0:128], pxt[0:w, 0:128], AF.Exp, [pxt], [lat], scale=-1.0)
                    act(lat[0:w, 0:128], f32(lat[0:w, 0:128]), AF.Ln, [lat], [lat], bias=1.0)
                    prv = P.ps()
                    um = cst["umat_" + kind]
                    mm(prv[0:w, 0:128], um[0:w, 0:w], lat[0:w, 0:128], True, True, [um, lat], [prv])
                    erv = sc()
                    act(erv[0:w, 0:128], prv[0:w, 0:128], AF.Exp, [prv], [erv], scale=-1.0)
                    kh = scr()
                    if state_only:
                        pk_sb = scr()
                        cp(pk_sb[0:w, 0:128], pkt[0:w, 0:128], [pkt], [pk_sb], eng="act")
                        yield
                        tt(kh[0:w, 0:128], f32(pk_sb[0:w, 0:128]), erv[0:w, 0:128], ALU.mult, [pk_sb, erv], [kh])
                    else:
                        tt(kh[0:w, 0:128], pkt[0:w, 0:128], erv[0:w, 0:128], ALU.mult, [pkt, erv], [kh])
                    S_in = []
                    if kind == "p":
                        cur = Sst[h]
                        for bi_ in range(nb):
                            S_in.append(cur)
                            psn = P.ps()
                            r0 = bi_ * blk
                            mm(psn[:, 0:256], kh[r0:r0 + blk, 0:128], vt[r0:r0 + blk, 0:256], True, True, [kh, vt], [psn])
                            dcol = c0 + r0 + blk - 1
                            nxt = tmw()
                            stt(nxt[:, 0:256], f32(cur[:, 0:256]), g_eb[:, dcol:dcol + 1], psn[:, 0:256], ALU.mult, ALU.add,
                                [cur, g_eb, psn], [nxt])
                            cur = nxt
                        S_fin = cur
                    else:
                        for s in range(nb):
                            si = tmw()
                            load(si, si[:, 0:256], gla_s[s0 + s, h, :, :], queue="pool")
                            S_in.append(si)
                            khm = scr()
                            ts(khm[0:w, 0:128], f32(kh[0:w, 0:128]), cst["bsel_s"][0:w, s:s + 1], None, ALU.mult, ALU.bypass,
                               [kh, cst["bsel_s"]], [khm])
                            psn = P.ps()
                            mm(psn[:, 0:256], khm[0:w, 0:128], vt[0:w, 0:256], True, True, [khm, vt], [psn])
                            so = stg()
                            dcol = c0 + s * blk + blk - 1
                            stt(so[:, 0:256], f32(si[:, 0:256]), g_eb[:, dcol:dcol + 1], psn[:, 0:256], ALU.mult, ALU.add,
                                [si, g_eb, psn], [so])
                            store(so, o_glas[s0 + s, h, :, :], so[:, 0:256])
                    if not state_only:
                        pat = P.ps()
                        mm(pat[0:w, 0:w], g_kt[:, cs_], g_qt[:, cs_], True, True, [g_kt_t, g_qt_t], [pat])
                        at = scr()
                        am = cst["amask_" + kind]
                        tt(at[0:w, 0:w], pat[0:w, 0:w], am[0:w, 0:w], ALU.mult, [pat, am], [at])
                        for ec in range(2):
                            po = P.ps()
                            mm(po[:, 0:w], vt[0:w, ec * 128:(ec + 1) * 128], at[0:w, 0:w], True, False, [vt, at], [po])
                            for bi_ in range(nb):
                                c0b = bi_ * blk
                                mm(po[:, c0b:c0b + blk], S_in[bi_][:, ec * 128:(ec + 1) * 128],
                                   g_qt[:, c0 + c0b:c0 + c0b + blk], False, bi_ == nb - 1, [S_in[bi_], g_qt_t], [po])
                            cp(g_po[ec][:, cs_], po[:, 0:w], [po], [g_po[ec]], eng="act")
                    if kind == "p":
                        cp(Sst[h][:, 0:256], f32(S_fin[:, 0:256]), [S_fin], [Sst[h]])
                    yield
                if state_only:
                    continue
                pss = P.ps()
                for ec in range(2):
                    sq = scr()
                    act(sq[:, 0:n], g_po[ec][:, 0:n], AF.Square, [g_po[ec]], [sq])
                    mm(pss[:, 0:n], ones_r[:, :], sq[:, 0:n], ec == 0, ec == 1, [ones_r, sq], [pss])
                ts(g_rs[:, 0:n], pss[:, 0:n], 1.0 / 256.0, EPS, ALU.mult, ALU.add, [pss], [g_rs])
                act(g_rs[:, 0:n], g_rs[:, 0:n], AF.Sqrt, [g_rs], [g_rs])
                recip(g_rs[:, 0:n], g_rs[:, 0:n], [g_rs], [g_rs])
                trr = wload(w_in, 0, 16, C_R + h * 256, 256)
                for ec in range(2):
                    pr_ = P.ps()
                    for k in range(16):
                        mm(pr_[:, 0:n], wv(trr, k, 256, ec * 128, 128), hT[k][:, 0:n], k == 0, k == 15, [trr, hT[k]], [pr_])
                    sr = sc()
                    act(sr[:, 0:n], pr_[:, 0:n], AF.Silu, [pr_], [sr])
                    o2 = sc()
                    stt(o2[:, 0:n], g_po[ec][:, 0:n], vec[:, GLN + h * 2 + ec:GLN + h * 2 + ec + 1], g_rs[:, 0:n],
                        ALU.mult, ALU.mult, [g_po[ec], vec, g_rs], [o2])
                    tt(ybr[h * 2 + ec][:, 0:n], o2[:, 0:n], sr[:, 0:n], ALU.mult, [o2, sr], [ybr_t[h * 2 + ec]])

        def gla_branch(n, tiles, s0, state_only):
            for _ in gla_gen(n, tiles, s0, state_only):
                pass

        def attn_seq(kd, vd, qx, c0, nq, extra=()):
            kn = kvk
            load(kn, kn[:, :].rearrange("p (m c) -> p m c", m=2), kd.rearrange("(m p) c -> p m c", p=128), extra=extra)
            vn = {}
            for mc in range(2):
                for h in range(4):
                    vt_ = tmw()
                    load(vt_, vt_[:, 0:256], vd[mc * 128:(mc + 1) * 128, h * 256:(h + 1) * 256], queue="pool", extra=extra)
                    vn[(mc, h)] = vt_
            for j in range(8):
                pb = P.ps()
                for mc in range(2):
                    tr(pb[:, mc * 128:(mc + 1) * 128], kn[:, mc * 1024 + j * 128:mc * 1024 + (j + 1) * 128], [kn], [pb])
                cp(mkT[:, j * 256:(j + 1) * 256], pb[:, 0:256], [pb], [mkT], eng="act" if j % 2 else "dve")
            for h in range(4):
                pT = []
                for mc in range(2):
                    psc = P.ps()
                    for hdc in range(2):
                        j = h * 2 + hdc
                        mm(psc[:, 0:nq], mkT[:, j * 256 + mc * 128:j * 256 + (mc + 1) * 128], qx[j][:, c0:c0 + nq],
                           hdc == 0, hdc == 1, [mkT, ygT_t[j]], [psc])
                    pt_ = scr()
                    act(pt_[:, 0:nq], psc[:, 0:nq], AF.Exp, [psc], [pt_], scale=1.0 / 16.0)
                    pT.append(pt_)
                pd = P.ps()
                for mc in range(2):
                    mm(pd[:, 0:nq], ones_r[:, :], pT[mc][:, 0:nq], mc == 0, mc == 1, [ones_r, pT[mc]], [pd])
                rd = sc()
                recip(rd[:, 0:nq], pd[:, 0:nq], [pd], [rd])
                for ec in range(2):
                    po = P.ps()
                    for mc in range(2):
                        mm(po[:, 0:nq], vn[(mc, h)][:, ec * 128:(ec + 1) * 128],
                           pT[mc][:, 0:nq], mc == 0, mc == 1, [vn[(mc, h)], pT[mc]], [po])
                    tt(ybr2[h * 2 + ec][:, c0:c0 + nq], po[:, 0:nq], rd[:, 0:nq], ALU.mult, [po, rd], [ybr2[h * 2 + ec]])

        memstores = []

        def xattn_branch(n, nP, s0, ns):
            qx = ygT

            def got_q(b, pt):
                cp(qx[b][:, 0:n], pt[:, 0:n], [pt], [ygT_t[b]], eng="act" if b % 2 else "dve")
            proj_fm(w_in, C_QX, 1024, n, hT, D, got_q)
            if nP:
                attn_seq(o_mk, o_mv, qx, 0, nP, extra=memstores)
            for s in range(ns):
                attn_seq(ck[s0 + s], cv[s0 + s], qx, nP + s * 8, 8)

        def mem_kv():
            SUB = int(os.environ.get("KSUB", "9"))
            load_x_group([(memp, 0, 0, 128), (memp, 128, 128, 128)])
            if SUB < 1:
                return
            norm_to_hT(256, xT, GM, hT)
            if SUB < 2:
                return
            for (W, od) in ((w_mk, o_mk), (w_mv, o_mv)):
                for q4 in range(4):
                    t = wload(W, 0, 16, q4 * 256, 256)
                    for mt in range(2):
                        pt = P.ps()
                        for k in range(16):
                            mm(pt[:, 0:256], hT[k][:, mt * 128:(mt + 1) * 128], wv(t, k, 256, 0, 256), k == 0, k == 15,
                               [t, hT[k]], [pt])
                        so = stg()
                        cp(so[:, 0:256], pt[:, 0:256], [pt], [so])
                        memstores.append(store(so, od[mt * 128:(mt + 1) * 128, q4 * 256:(q4 + 1) * 256], so[:, 0:256]))

        def tail(n, xtiles, outs):
            for s0_ in range(0, D, 256):
                for bb in range(2):
                    dc = s0_ // 128 + bb
                    t = wload(w_out, 0, 16, dc * 128, 128)
                    pt = P.ps()
                    for k in range(16):
                        mm(pt[:, 0:n], wv(t, k, 128, 0, 128), mg[k][:, 0:n], k == 0, k == 15, [t, mg[k]], [pt])
                    cp(xT[dc][:, 0:n], pt[:, 0:n], [pt], [xT[dc]], eng="act" if bb else "dve")
            load_x_group(xtiles, accumulate=True)
            norm_to_hT(n, xT, G2, hT)
            HS = 256
            for s0_ in range(0, FFN, HS):
                acts = []
                pgs = [P.ps(), P.ps()]; pus = [P.ps(), P.ps()]
                for half in range(2):
                    tg = wload(w_fg, half * 8, 8, s0_, HS)
                    tu = wload(w_fu, half * 8, 8, s0_, HS)
                    for hc in range(HS // 128):
                        for k in range(8):
                            mm(pgs[hc][:, 0:n], wv(tg, k, HS, hc * 128, 128), hT[half * 8 + k][:, 0:n],
                               half == 0 and k == 0, half == 1 and k == 7, [tg, hT[half * 8 + k]], [pgs[hc]])
                        for k in range(8):
                            mm(pus[hc][:, 0:n], wv(tu, k, HS, hc * 128, 128), hT[half * 8 + k][:, 0:n],
                               half == 0 and k == 0, half == 1 and k == 7, [tu, hT[half * 8 + k]], [pus[hc]])
                for hc in range(HS // 128):
                    sg = sc()
                    act(sg[:, 0:n], pgs[hc][:, 0:n], AF.Silu, [pgs[hc]], [sg])
                    a_ = scr()
                    tt(a_[:, 0:n], sg[:, 0:n], pus[hc][:, 0:n], ALU.mult, [sg, pus[hc]], [a_])
                    acts.append(a_)
                tds = [wload(w_fd, s0_ // 128 + hc, 1, 0, D) for hc in range(HS // 128)]
                for dc in range(NCH):
                    pd = P.ps()
                    for hc in range(HS // 128):
                        mm(pd[:, 0:n], wv(tds[hc], 0, D, dc * 128, 128), acts[hc][:, 0:n], hc == 0, hc == HS // 128 - 1,
                           [tds[hc], acts[hc]], [pd])
                    tt(xT[dc][:, 0:n], f32(xT[dc][:, 0:n]), pd[:, 0:n], ALU.add, [xT[dc], pd], [xT[dc]])
            pq = P.ps()
            for c in range(NCH):
                sq = scr()
                act(sq[:, 0:n], f32(xT[c][:, 0:n]), AF.Square, [xT[c]], [sq])
                mm(pq[:, 0:n], ones_r[:, :], sq[:, 0:n], c == 0, c == NCH - 1, [ones_r, sq], [pq])
            ts(rstd[:, 0:n], pq[:, 0:n], 1.0 / D, EPS, ALU.mult, ALU.add, [pq], [rstd])
            act(rstd[:, 0:n], rstd[:, 0:n], AF.Sqrt, [rstd], [rstd])
            recip(rstd[:, 0:n], rstd[:, 0:n], [rstd], [rstd])
            for c in range(NCH):
                stt(xT[c][:, 0:n], f32(xT[c][:, 0:n]), vec[:, GF + c:GF + c + 1], rstd[:, 0:n], ALU.mult, ALU.mult,
                    [xT[c], vec, rstd], [xT[c]])
            for (od, r0, c0, w) in outs:
                for q4 in range(4):
                    pb = P.ps()
                    for j in range(4):
                        c = q4 * 4 + j
                        tr(pb[0:w, j * 128:(j + 1) * 128], f32(xT[c][:, c0:c0 + w]), [xT[c]], [pb])
                    ot = stg()
                    cp(ot[0:w, 0:512], pb[0:w, 0:512], [pb], [ot], eng="act" if q4 % 2 else "dve")
                    store(ot, od[r0:r0 + w, q4 * 512:(q4 + 1) * 512], ot[0:w, 0:512])

        STAGE = int(os.environ.get("KSTAGE", "99"))
        if STAGE >= 1:
            mem_kv()
        for g in range(4 if STAGE >= 2 else 0):
            xt = [(xpre, g * 256, 0, 128), (xpre, g * 256 + 128, 128, 128)]
            load_x_group(xt)
            norm_to_hT(256, xT, G1, hT)
            gg = gla_gen(256, [("p", 0, 128), ("p", 128, 128)], 0, True)

            def drain_g(gg=gg):
                for _ in gg:
                    pass
            s5_branch(256, 256, 0, 0, True, hook=lambda gg=gg: next(gg, None), drain=drain_g)
        NGRP = int(os.environ.get("KGROUPS", "4"))
        for g in range(NGRP if STAGE >= 3 else 0):
            n = NG
            xt = [(xp, g * 256, 0, 128), (xp, g * 256 + 128, 128, 128), (xs, g * 32, 256, 32)]
            outs = [(o_yp, g * 256, 0, 128), (o_yp, g * 256 + 128, 128, 128), (o_ys, g * 32, 256, 32)]
            load_x_group(xt)
            norm_to_hT(n, xT, G1, hT)
            gla_branch(n, [("p", 0, 128), ("p", 128, 128), ("s", 256, 32)], g * 4, False)
            xattn_branch(n, 256, g * 4, 4)
            for ri, src_ in enumerate((s5re_s, s5im_s)):
                st_ = stg()
                load(st_, st_[:, 0:128], src_[g * 128:(g + 1) * 128, :])
                pb = P.ps()
                tr(pb[:, 0:128], st_[:, 0:128], [st_], [pb])
                tt(s5h0[:, ri * 128:(ri + 1) * 128].rearrange("p (s g) -> p s g", g=NGP),
                   pb[:, 0:128].rearrange("p (s g) -> p s g", g=NGP),
                   amag[:, :].unsqueeze(1).to_broadcast([128, 4, NGP]), ALU.mult, [pb, amag], [s5h0])

            def _chain(n=n):
                for _ in merge_gen(1, n, True):
                    yield
                for _ in merge_gen(2, n, False, src=ybr2, srct=ybr2):
                    yield
            mgn = _chain()

            def drain_m(mgn=mgn):
                for _ in mgn:
                    pass
            s5_branch(n, 256, g * 4, 4, False, hook=lambda mgn=mgn: next(mgn, None), drain=drain_m)
            for ri in range(2):
                pb = P.ps()
                tr(pb[:, 0:128], s5hf[:, ri * 128:(ri + 1) * 128], [s5hf], [pb])
                so = stg()
                cp(so[:, 0:128], pb[:, 0:128], [pb], [so])
                store(so, o_s5s[ri, g * 128:(g + 1) * 128, :], so[:, 0:128])
            merge_branch(0, n, False)
            if STAGE >= 7:
                tail(n, xt, outs)
        for ri in range(2):
            pb = P.ps()
            tr(pb[0:NGP, 0:128], s5c[:, ri * NGP:(ri + 1) * NGP], [s5c], [pb])
            so = stg()
            cp(so[0:NGP, 0:128], pb[0:NGP, 0:128], [pb], [so])
            store(so, o_s5p[ri, :, :], so[0:NGP, 0:128])
        for h in range(4):
            so = stg()
            cp(so[:, 0:256], f32(Sst[h][:, 0:256]), [Sst[h]], [so])
            store(so, o_glap[h, :, :], so[:, 0:256])

        P.emit()
    return nc


_NC_CACHE = {}


def _host_layouts(inp):
    f = np.float32
    shared = {}
    shared["w_in"] = np.ascontiguousarray(inp["w_in"][0])
    shared["w_glu"] = np.ascontiguousarray(inp["s5_w_glu"][0])
    shared["w_mk"] = np.ascontiguousarray(inp["w_mem_k"][0])
    shared["w_mv"] = np.ascontiguousarray(inp["w_mem_v"][0])
    shared["w_br0"] = np.ascontiguousarray(inp["w_br_s5"][0])
    shared["w_br1"] = np.ascontiguousarray(inp["w_br_gla"][0])
    shared["w_br2"] = np.ascontiguousarray(inp["w_br_xattn"][0])
    shared["w_out"] = np.ascontiguousarray(inp["w_out"][0])
    shared["w_fg"] = np.ascontiguousarray(inp["w_ffn_gate"][0])
    shared["w_fu"] = np.ascontiguousarray(inp["w_ffn_up"][0])
    shared["w_fd"] = np.ascontiguousarray(inp["w_ffn_down"][0])
    shared["w_a2aug"] = np.ascontiguousarray(np.concatenate([inp["gla_w_a2"][0], inp["gla_b_a"][0][None, :]], 0)).astype(f)
    vecs = np.zeros((128, 96), f)

    def colz(v):
        return np.ascontiguousarray(v.reshape(-1, 128).T)
    vecs[:, 0:16] = colz(inp["norm_mix"][0]); vecs[:, 16:32] = colz(inp["norm_ffn"][0])
    vecs[:, 32:48] = colz(inp["norm_final"]); vecs[:, 48:64] = colz(inp["mem_norm"][0])
    vecs[:, 64:72] = colz(inp["gla_norm"][0]); vecs[:, 72:80] = colz(inp["s5_d"][0].reshape(-1))
    vecs[:, 80:88] = colz(inp["s5_b_glu"][0])
    shared["vecs"] = vecs

    def qlay(a):
        return np.ascontiguousarray(a.reshape(NGP, 2, PST).transpose(1, 2, 0).reshape(128, NGP))
    shared["lamre"] = qlay(inp["s5_lam_re"][0]); shared["lamim"] = qlay(inp["s5_lam_im"][0])
    shared["logdt"] = qlay(np.broadcast_to(inp["s5_log_dt"][0][:, None], (G, PST)))
    btc = np.zeros((8, 128, 256), f)
    for ri, src in enumerate((inp["s5_b_re"][0], inp["s5_b_im"][0])):
        for gp in range(NGP):
            uc, g4 = gp // 4, gp % 4
            for two in range(2):
                btc[uc, g4 * 32 + two * 16:g4 * 32 + two * 16 + 16, ri * 128 + two * 64:ri * 128 + (two + 1) * 64] = src[2 * gp + two].T
    shared["btc"] = btc
    ctp = np.zeros((NGP, 128, 256), f)
    for ri, src in enumerate((inp["s5_c_re"][0], inp["s5_c_im"][0])):
        for gp in range(NGP):
            g4 = gp % 4
            for two in range(2):
                ctp[gp, two * 64:(two + 1) * 64, ri * 128 + g4 * 32 + two * 16:ri * 128 + g4 * 32 + two * 16 + 16] = src[2 * gp + two].T
    shared["ctp"] = ctp
    for k, v in _consts().items():
        shared["c_" + k] = v
    return shared


def kernel(**inp):
    inp = {k: np.asarray(v) for k, v in inp.items()}
    if "nc" not in _NC_CACHE:
        _NC_CACHE["nc"] = build_program()
    nc = _NC_CACHE["nc"]
    shared = _host_layouts(inp)
    xpr = inp["x_prompt"]; xsm = inp["x_sample"]
    in_maps = []
    for c in range(8):
        b, half = c // 2, c % 2
        m = dict(shared)
        m["xp"] = np.ascontiguousarray(xpr[b, half * 1024:(half + 1) * 1024])
        m["xpre"] = np.ascontiguousarray(xpr[b, 0:1024]) if half == 1 else np.zeros((1024, D), np.float32)
        sq = slice(16 * c, 16 * c + 16)
        m["xs"] = np.ascontiguousarray(xsm[sq].reshape(128, D))
        m["memp"] = np.ascontiguousarray(inp["mem_prompt"][b])
        m["s5re_s"] = np.ascontiguousarray(inp["state_s5_re"][0, sq].reshape(512, 128))
        m["s5im_s"] = np.ascontiguousarray(inp["state_s5_im"][0, sq].reshape(512, 128))
        m["gla_s"] = np.ascontiguousarray(inp["state_gla"][0, sq])
        m["ck"] = np.ascontiguousarray(inp["cache_mem_k"][0, sq].reshape(16, 256, 1024))
        m["cv"] = np.ascontiguousarray(inp["cache_mem_v"][0, sq].reshape(16, 256, 1024))
        in_maps.append(m)
    NCORE = int(os.environ.get("KCORES", "8"))
    res = run_bass_kernel_spmd(nc, in_maps[:NCORE], core_ids=list(range(NCORE)))
    R = list(res.results) + [None] * (8 - NCORE)
    f = np.float32
    y_prompt = np.zeros((4, 2048, D), f); y_sample = np.zeros((128, 8, D), f)
    s5p_re = np.zeros((1, 4, G, PST), f); s5p_im = np.zeros((1, 4, G, PST), f)
    glap = np.zeros((1, 4, 4, 128, 256), f)
    mk = np.zeros((1, 4, 256, 4, 256), f); mv = np.zeros((1, 4, 256, 4, 256), f)
    s5s_re = np.zeros((1, 128, G, PST), f); s5s_im = np.zeros((1, 128, G, PST), f)
    glas = np.zeros((1, 128, 4, 128, 256), f)
    for c in range(8):
        b, half = c // 2, c % 2
        r = R[c]
        if r is None:
            continue
        y_prompt[b, half * 1024:(half + 1) * 1024] = r["o_yp"]
        sq = slice(16 * c, 16 * c + 16)
        y_sample[sq] = r["o_ys"].reshape(16, 8, D)
        s5s_re[0, sq] = r["o_s5s"][0].reshape(16, G, PST)
        s5s_im[0, sq] = r["o_s5s"][1].reshape(16, G, PST)
        glas[0, sq] = r["o_glas"]
        if half == 1:
            s5p_re[0, b] = r["o_s5p"][0].reshape(G, PST)
            s5p_im[0, b] = r["o_s5p"][1].reshape(G, PST)
            glap[0, b] = r["o_glap"]
        else:
            mk[0, b] = r["o_mk"].reshape(256, 4, 256)
            mv[0, b] = r["o_mv"].reshape(256, 4, 256)
    return (y_prompt, y_sample, s5p_re, s5p_im, glap, mk, mv, s5s_re, s5s_im, glas)
```

```python
import math
import os
from contextlib import ExitStack
import numpy as np
import ml_dtypes
import concourse.bass as bass
import concourse.mybir as mybir
from concourse.bass_utils import run_bass_kernel_spmd

F32 = mybir.dt.float32
F32R = mybir.dt.float32r
BF16 = mybir.dt.bfloat16
AF = mybir.ActivationFunctionType
ALU = mybir.AluOpType

D = 2048
NCH = 16
G = 64
PST = 64
NGP = 32
FFN = 5632
IN_W = 11280
C_S5, C_Q, C_K, C_V, C_R, C_A, C_QX, C_GATE = 0, 1024, 1536, 2048, 3072, 4096, 4112, 5136
NG = 288
TS = 64
EPS = 1e-6
SAME_ENGINE_SYNC = True


class T:
    __slots__ = ("h", "name", "last_w", "readers", "sem", "ndma")

    def __init__(self, h, name):
        self.h = h
        self.name = name
        self.last_w = None
        self.readers = []
        self.sem = None
        self.ndma = 0

    def __getitem__(self, k):
        return self.h[k]


class Op:
    __slots__ = ("eng", "fn", "deps", "sig", "idx", "dma_tile", "dma_cum", "sigidx")

    def __init__(self, eng, fn):
        self.eng = eng
        self.fn = fn
        self.deps = []
        self.sig = False
        self.dma_tile = None
        self.dma_cum = 0
        self.sigidx = 0


class Prog:
    ENGS = ("pe", "act", "dve", "pool", "sp")

    def __init__(self, nc, es):
        self.nc = nc
        self.es = es
        self.q = {e: [] for e in self.ENGS}
        self.tiles = []
        self.nps = 0
        self.psb = []

    def sb(self, name, shape, dt):
        h = self.es.enter_context(self.nc.sbuf_tensor(name, list(shape), dt))
        t = T(h, name)
        self.tiles.append(t)
        return t

    def init_psum(self):
        self.psall = self.es.enter_context(self.nc.psum_tensor("psall", [128, 4096], F32))
        for i in range(8):
            t = T(self.psall[:, i * 512:(i + 1) * 512], "psb%d" % i)
            self.tiles.append(t)
            self.psb.append(t)
        self.ps_list = list(range(8))

    def ps(self):
        t = self.psb[self.ps_list[self.nps % len(self.ps_list)]]
        self.nps += 1
        return t

    def _rec(self, op, reads, writes, extra=()):
        def _unw(lst):
            out = []
            for r in lst:
                if hasattr(r, "ts"):
                    out.extend(r.ts)
                else:
                    out.append(getattr(r, "t", r))
            return out
        reads = _unw(reads)
        writes = _unw(writes)
        pr_ = [r for r in reads if r.name.startswith("psb")]
        if pr_:
            writes = list(writes) + [r for r in pr_ if r not in writes]
            reads = [r for r in reads if not r.name.startswith("psb")]
        deps = list(extra)
        for r in reads:
            if r.last_w is not None:
                deps.append(r.last_w)
        for w in writes:
            if w.last_w is not None:
                deps.append(w.last_w)
            deps.extend(w.readers)
        seen = set()
        for d in deps:
            if id(d) in seen or d is op:
                continue
            seen.add(id(d))
            op.deps.append(d)
            if d.dma_tile is None and not (d.eng == op.eng and (op.eng == "pe" or not SAME_ENGINE_SYNC)):
                d.sig = True
        for r in reads:
            r.readers.append(op)
        for w in writes:
            w.last_w = op
            w.readers = []
        self.q[op.eng].append(op)

    def op(self, eng, fn, reads=(), writes=()):
        o = Op(eng, fn)
        self._rec(o, reads, writes)
        return o

    def dma(self, queue, fn, tile, reads=(), writes=(), extra=()):
        o = Op(queue, fn)
        o.dma_tile = tile
        tile.ndma += 1
        o.dma_cum = tile.ndma
        self._rec(o, reads, writes, extra)
        return o

    def emit(self):
        nc = self.nc
        es = self.es
        esem = {e: es.enter_context(nc.semaphore("sem_" + e)) for e in self.ENGS}
        for t in self.tiles:
            if t.ndma > 0:
                t.sem = es.enter_context(nc.semaphore("d_" + t.name))
        fin = es.enter_context(nc.semaphore("fin"))
        for e in self.ENGS:
            n = 0
            for o in self.q[e]:
                if o.dma_tile is None and o.sig:
                    n += 1
                    o.sigidx = n
        block = es.enter_context(nc.Block())
        finals = {}
        for e in self.ENGS:
            ops = self.q[e]
            finals[e] = max([o.sigidx for o in ops] + [0])

        def run(e, eng):
            seen = {}
            for o in self.q[e]:
                waits = {}
                for d in o.deps:
                    if d.dma_tile is not None:
                        s, v = d.dma_tile.sem, 16 * d.dma_cum
                    else:
                        if d.eng == e and (e == "pe" or not SAME_ENGINE_SYNC):
                            continue
                        s, v = esem[d.eng], d.sigidx
                    k = id(s)
                    if k not in waits or waits[k][1] < v:
                        waits[k] = (s, v)
                for k, (s, v) in waits.items():
                    if seen.get(k, 0) >= v:
                        continue
                    seen[k] = v
                    eng.wait_ge(s, v)
                ins = o.fn(eng)
                if o.dma_tile is not None:
                    ins.then_inc(o.dma_tile.sem, 16)
                elif o.sig:
                    ins.then_inc(esem[e], 1)
            if e == "sp":
                for e2 in self.ENGS:
                    if e2 != "sp" and finals[e2] > 0:
                        eng.wait_ge(esem[e2], finals[e2])
                for t in self.tiles:
                    if t.ndma > 0:
                        eng.wait_ge(t.sem, 16 * t.ndma)

        @block.tensor
        def _(eng):
            run("pe", eng)

        @block.scalar
        def _(eng):
            run("act", eng)

        @block.vector
        def _(eng):
            run("dve", eng)

        @block.gpsimd
        def _(eng):
            run("pool", eng)

        @block.sync
        def _(eng):
            run("sp", eng)


def _consts():
    c = {}
    c["ident"] = np.eye(128, dtype=np.float32)
    c["ones"] = np.ones((128, 128), np.float32)
    idx = np.arange(128)
    for nm, blk in (("p", 64), ("s", 8)):
        same = (idx[:, None] // blk) == (idx[None, :] // blk)
        c["amask_" + nm] = (same & (idx[:, None] <= idx[None, :])).astype(np.float32)
        c["umat_" + nm] = (same & (idx[:, None] > idx[None, :])).astype(np.float32) / 16.0
        c["rmask_" + nm] = np.broadcast_to(((idx % blk) != 0).astype(np.float32)[None, :], (128, 128)).copy()
        nb = 128 // blk
        c["bsel_" + nm] = ((idx[:, None] // blk) == np.arange(nb)[None, :]).astype(np.float32)
    c["cmask"] = np.concatenate([np.broadcast_to(((idx // 32) == g4).astype(np.float32)[None, :], (128, 128))
                                 for g4 in range(4)], axis=1).copy()
    c["rowmask"] = ((idx[:, None] // 32) == np.arange(4)[None, :]).astype(np.float32)
    return c


def build_program():
    nc = bass.Bass("TRN2", target_bir_lowering=False)
    dram = {}

    def din(name, shape, dt=F32):
        dram[name] = nc.dram_tensor(name, list(shape), dt, kind="ExternalInput").ap()
        return dram[name]

    def dout(name, shape):
        dram[name] = nc.dram_tensor(name, list(shape), F32, kind="ExternalOutput").ap()
        return dram[name]

    xp = din("xp", [1024, D]); xpre = din("xpre", [1024, D]); xs = din("xs", [128, D])
    memp = din("memp", [256, D])
    s5re_s = din("s5re_s", [512, 128]); s5im_s = din("s5im_s", [512, 128])
    gla_s = din("gla_s", [16, 4, 128, 256])
    ck = din("ck", [16, 256, 1024]); cv = din("cv", [16, 256, 1024])
    w_in = din("w_in", [D, IN_W]); w_glu = din("w_glu", [1024, 1024])
    w_mk = din("w_mk", [D, 1024]); w_mv = din("w_mv", [D, 1024])
    w_br = [din("w_br%d" % i, [1024, D]) for i in range(3)]
    w_out = din("w_out", [D, D])
    w_fg = din("w_fg", [D, FFN]); w_fu = din("w_fu", [D, FFN]); w_fd = din("w_fd", [FFN, D])
    w_a2 = din("w_a2aug", [17, 512])
    vecs = din("vecs", [128, 96])
    lamre = din("lamre", [128, NGP]); lamim = din("lamim", [128, NGP]); logdt = din("logdt", [128, NGP])
    btc = din("btc", [8, 128, 256])
    ctp = din("ctp", [NGP, 128, 256])
    CONSTS = _consts()
    cn = {k: din("c_" + k, list(v.shape)) for k, v in CONSTS.items()}

    o_yp = dout("o_yp", [1024, D]); o_ys = dout("o_ys", [128, D])
    o_s5p = dout("o_s5p", [2, NGP, 128])
    o_glap = dout("o_glap", [4, 128, 256])
    o_mk = dout("o_mk", [256, 1024]); o_mv = dout("o_mv", [256, 1024])
    o_s5s = dout("o_s5s", [2, 512, 128])
    o_glas = dout("o_glas", [16, 4, 128, 256])

    with ExitStack() as es:
        P = Prog(nc, es)
        P.init_psum()

        def pool_of(name, n, shape, dt):
            tl = [P.sb("%s%d" % (name, i), shape, dt) for i in range(n)]
            cnt = [0]

            def nxt():
                t = tl[cnt[0] % n]
                cnt[0] += 1
                return t
            return nxt

        ident = P.sb("ident", [128, 128], F32)
        ones_r = P.sb("ones_r", [128, 128], F32R)
        cst = {}
        for k, v in CONSTS.items():
            if k in ("ident", "ones"):
                continue
            cst[k] = P.sb("k_" + k, [128, v.shape[1]], F32R if k.startswith("umat") else F32)
        vec = P.sb("vec", [128, 96], F32)
        wa2 = P.sb("wa2", [32, 512], F32R)
        cosT = P.sb("cosT", [128, NGP * TS], F32)
        sinT = P.sb("sinT", [128, NGP * TS], F32)
        amag = P.sb("amag", [128, NGP], F32)
        bbc = P.sb("bbc", [128, 8 * 256], BF16)
        s5c = P.sb("s5c", [128, 2 * NGP], F32)
        s5h0 = P.sb("s5h0", [128, 2 * 128], F32)
        s5hf = P.sb("s5hf", [128, 2 * 128], F32)
        Sst = [P.sb("Sst%d" % h, [128, 256], F32R) for h in range(4)]
        hT = [P.sb("hT%d" % c, [128, NG], F32R) for c in range(NCH)]
        mg = [P.sb("mg%d" % c, [128, NG], F32R) for c in range(NCH)]
        xT = [P.sb("xT%d" % c, [128, NG], F32R) for c in range(NCH)]

        class RV:
            def __init__(self, t):
                self.t = t

            def __getitem__(self, k):
                return self.t.h[k].bitcast(F32R)
        ybr = [RV(xT[c]) for c in range(8)]
        ygT = [RV(xT[8 + c]) for c in range(8)]
        ybr2 = [P.sb("ybr2_%d" % c, [128, NG], F32R) for c in range(8)]
        ybr_t = [xT[c] for c in range(8)]
        ygT_t = [xT[8 + c] for c in range(8)]

        def f32(ap):
            return ap.bitcast(F32)
        rstd = P.sb("rstd", [128, NG], F32)
        NSLOT = 4
        SLW = 2048
        wslA = es.enter_context(nc.sbuf_tensor("wslA", [128, NSLOT * SLW], F32R))
        wsl = [T(wslA[:, i * SLW:(i + 1) * SLW], "wsl%d" % i) for i in range(NSLOT)]
        P.tiles.extend(wsl)
        wcnt = [0]

        class WS:
            def __init__(self, i0, nu):
                self.ts = [wsl[i0 + j] for j in range(nu)]
                self.o = i0 * SLW

            def __getitem__(self, key):
                p, c = key
                return wslA[p, c.start + self.o:c.stop + self.o]
        kvk = P.sb("kvk", [128, 2048], F32)
        mkT = P.sb("mkT", [128, 8 * 256], F32R)
        alow = P.sb("alow", [32, NG], F32R)
        sc = pool_of("sc", 4, [128, NG], F32)
        mgs = pool_of("mgs", 2, [128, NG], F32)
        big2 = [P.sb("big%d" % i, [128, 2 * NG], F32) for i in range(6)]

        class HV:
            def __init__(self, t, half):
                self.t = t
                self.o = half * NG

            def __getitem__(self, k):
                p, c = k
                c0 = (c.start or 0) + self.o
                c1 = (c.stop if c.stop is not None else NG) + self.o
                return self.t.h[p, c0:c1]
        bigc = [0]

        def s5t():
            t = big2[bigc[0] % 6]
            bigc[0] += 1
            return t
        s5u = pool_of("s5u", 2, [128, NG], F32)
        scr = pool_of("scr", 6, [128, NG], F32R)
        scb = pool_of("scb", 3, [128, 2 * NG], BF16)
        s5ub = pool_of("s5ub", 2, [128, NG], BF16)
        bpad = pool_of("bpad", 2, [128, 256], BF16)
        cpad = pool_of("cpad", 2, [128, 256], BF16)
        sm = pool_of("sm", 24, [128, 32], F32)
        cfp = pool_of("cfp", 3, [128, 2 * TS + 64], F32)
        tmw = pool_of("tmw", 10, [128, 256], F32R)
        stg = pool_of("stg", 3, [128, 512], F32)
        g_la0, g_b0, g_eb0, g_enb, g_rm0, g_rs = (HV(big2[0], 0), HV(big2[0], 1), HV(big2[1], 0), HV(big2[1], 1),
                                                  HV(big2[2], 0), HV(big2[2], 1))
        g_la_t, g_b_t, g_eb_t, g_enb_t, g_rm_t, g_rs_t = big2[0], big2[0], big2[1], big2[1], big2[2], big2[2]
        g_qt_t = P.sb("g_qt", [128, NG], F32R); g_kt_t = P.sb("g_kt", [128, NG], F32R)
        g_qt, g_kt = g_qt_t, g_kt_t
        g_po = [HV(big2[3], 0), HV(big2[3], 1)]
        g_po_t = [big2[3], big2[3]]
        ac63 = P.sb("ac63", [128, NGP], F32); as63 = P.sb("as63", [128, NGP], F32)
        print("SBUF bytes remaining per partition:", nc.sbuf_bytes_remaining)

        def mm(out_ap, lhsT, rhs, start, stop, r, w):
            P.op("pe", lambda e: e.matmul(out_ap, lhsT, rhs, start=start, stop=stop), reads=r, writes=w)

        def tr(out_ap, in_ap, r, w):
            k = in_ap.shape[0]
            P.op("pe", lambda e: e.transpose(out_ap, in_ap, ident[0:k, 0:k]), reads=list(r) + [ident], writes=w)

        def act(out_ap, in_ap, func, r, w, bias=None, scale=None):
            kw = {}
            if bias is not None:
                kw["bias"] = bias
            if scale is not None:
                kw["scale"] = scale
            P.op("act", lambda e: e.activation(out_ap, in_ap, func, **kw), reads=r, writes=w)

        def tt(out_ap, a, b, op, r, w, eng="dve"):
            P.op(eng, lambda e: e.tensor_tensor(out_ap, a, b, op), reads=r, writes=w)

        def ts(out_ap, a, s1, s2, op0, op1, r, w, eng="dve"):
            P.op(eng, lambda e: e.tensor_scalar(out_ap, a, s1, s2, op0, op1), reads=r, writes=w)

        def stt(out_ap, a, s_, b, op0, op1, r, w, eng="dve"):
            P.op(eng, lambda e: e.scalar_tensor_tensor(out_ap, a, s_, b, op0, op1), reads=r, writes=w)

        def cp(out_ap, in_ap, r, w, eng="dve"):
            if eng == "act":
                if os.environ.get("KACTCP", "ident") == "ident":
                    P.op("act", lambda e: e.activation(out_ap, in_ap, AF.Identity), reads=r, writes=w)
                else:
                    P.op("act", lambda e: e.copy(out_ap, in_ap), reads=r, writes=w)
            else:
                P.op(eng, lambda e: e.tensor_copy(out_ap, in_ap), reads=r, writes=w)

        def recip(out_ap, in_ap, r, w):
            P.op("dve", lambda e: e.reciprocal(out_ap, in_ap), reads=r, writes=w)

        def scan(out_ap, d0, d1, init, r, w):
            P.op("dve", lambda e: e.tensor_tensor_scan(out_ap, d0, d1, init, ALU.mult, ALU.add), reads=r, writes=w)

        def load(tile, out_ap, in_ap, queue="sp", extra=()):
            return P.dma(queue, lambda e: e.dma_start(out=out_ap, in_=in_ap), tile, writes=[tile], extra=extra)

        def store(tile, out_ap, in_ap, queue="sp"):
            return P.dma(queue, lambda e: e.dma_start(out=out_ap, in_=in_ap), tile, reads=[tile])

        def wload(W, k0, kc, c0, C):
            nu = (kc * C + SLW - 1) // SLW
            assert nu in (1, 2)
            if nu == 2 and wcnt[0] % 2:
                wcnt[0] += 1
            i0 = wcnt[0] % NSLOT
            wcnt[0] += nu
            ws = WS(i0, nu)
            o = ws[:, 0:kc * C].rearrange("p (k c) -> p k c", k=kc)
            i = W[k0 * 128:(k0 + kc) * 128, c0:c0 + C].rearrange("(k p) c -> p k c", p=128)
            P.dma("pool", lambda e: e.dma_start(out=o, in_=i), ws.ts[0], writes=list(ws.ts))
            return ws

        def wv(t, k, C, a, n):
            return t[:, k * C + a:k * C + a + n]

        load(ident, ident[:, :], cn["ident"][:, :])
        load(ones_r, ones_r[:, :], cn["ones"][:, :], queue="pool")
        for k, t in cst.items():
            load(t, t[:, :], cn[k][:, :], queue="pool" if k.startswith("umat") else "sp")
        load(vec, vec[:, :], vecs[:, :])
        load(wa2, wa2[0:17, :], w_a2[:, :], queue="pool")
        G1, G2, GF, GM, GLN, S5D, BGLU = 0, 16, 32, 48, 64, 72, 80

        lre = sm(); lim = sm(); ldt = sm()
        load(lre, lre[:, :], lamre[:, :]); load(lim, lim[:, :], lamim[:, :]); load(ldt, ldt[:, :], logdt[:, :])
        dtt = sm(); tmp = sm(); psi = sm(); p2 = sm(); sn = sm(); cs = sm(); t1 = sm(); t2 = sm()
        act(dtt[:, :], ldt[:, :], AF.Exp, [ldt], [dtt])
        tt(tmp[:, :], lre[:, :], dtt[:, :], ALU.mult, [lre, dtt], [tmp])
        act(amag[:, :], tmp[:, :], AF.Exp, [tmp], [amag])
        stt(psi[:, :], lim[:, :], 1.0 / 64.0, dtt[:, :], ALU.mult, ALU.mult, [lim, dtt], [psi])
        tt(p2[:, :], psi[:, :], psi[:, :], ALU.mult, [psi], [p2])
        ts(sn[:, :], p2[:, :], -1.0 / 42.0, 1.0, ALU.mult, ALU.add, [p2], [sn])
        tt(sn[:, :], sn[:, :], p2[:, :], ALU.mult, [sn, p2], [sn])
        ts(sn[:, :], sn[:, :], -1.0 / 20.0, 1.0, ALU.mult, ALU.add, [sn], [sn])
        tt(sn[:, :], sn[:, :], p2[:, :], ALU.mult, [sn, p2], [sn])
        ts(sn[:, :], sn[:, :], -1.0 / 6.0, 1.0, ALU.mult, ALU.add, [sn], [sn])
        tt(sn[:, :], sn[:, :], psi[:, :], ALU.mult, [sn, psi], [sn])
        ts(cs[:, :], p2[:, :], -1.0 / 56.0, 1.0, ALU.mult, ALU.add, [p2], [cs])
        tt(cs[:, :], cs[:, :], p2[:, :], ALU.mult, [cs, p2], [cs])
        ts(cs[:, :], cs[:, :], -1.0 / 30.0, 1.0, ALU.mult, ALU.add, [cs], [cs])
        tt(cs[:, :], cs[:, :], p2[:, :], ALU.mult, [cs, p2], [cs])
        ts(cs[:, :], cs[:, :], -1.0 / 12.0, 1.0, ALU.mult, ALU.add, [cs], [cs])
        tt(cs[:, :], cs[:, :], p2[:, :], ALU.mult, [cs, p2], [cs])
        ts(cs[:, :], cs[:, :], -0.5, 1.0, ALU.mult, ALU.add, [cs], [cs])
        for _ in range(6):
            tt(t1[:, :], cs[:, :], cs[:, :], ALU.mult, [cs], [t1])
            tt(t2[:, :], sn[:, :], sn[:, :], ALU.mult, [sn], [t2])
            stt(sn[:, :], sn[:, :], 2.0, cs[:, :], ALU.mult, ALU.mult, [sn, cs], [sn])
            tt(cs[:, :], t1[:, :], t2[:, :], ALU.subtract, [t1, t2], [cs])
        cos3 = cosT[:, :].rearrange("p (g j) -> p g j", j=TS)
        sin3 = sinT[:, :].rearrange("p (g j) -> p g j", j=TS)
        cp(cos3[:, :, 0], cs[:, :], [cs], [cosT])
        cp(sin3[:, :, 0], sn[:, :], [sn], [sinT])
        tA = kvk
        tBt = big2[5]
        n_ = 1
        while n_ < TS:
            for gh in range(2):
                gs = slice(gh * 16, (gh + 1) * 16)
                cr = cos3[:, gs, n_ - 1:n_].to_broadcast([128, 16, n_])
                sr = sin3[:, gs, n_ - 1:n_].to_broadcast([128, 16, n_])
                a3 = tA[:, 0:16 * n_].rearrange("p (g j) -> p g j", j=n_)
                b3 = tBt[:, 0:16 * n_].rearrange("p (g j) -> p g j", j=n_)
                tt(a3, cos3[:, gs, 0:n_], cr, ALU.mult, [cosT], [tA])
                tt(b3, sin3[:, gs, 0:n_], sr, ALU.mult, [sinT], [tBt])
                tt(a3, a3, b3, ALU.subtract, [tA, tBt], [tA])
                tt(b3, cos3[:, gs, 0:n_], sr, ALU.mult, [cosT, sinT], [tBt])
                cp(cos3[:, gs, n_:2 * n_], a3, [tA], [cosT])
                tt(a3, sin3[:, gs, 0:n_], cr, ALU.mult, [sinT, cosT], [tA])
                tt(sin3[:, gs, n_:2 * n_], a3, b3, ALU.add, [tA, tBt], [sinT])
            n_ *= 2
        are = sm(); aim = sm(); den = sm(); cre = sm(); cim = sm()
        tt(are[:, :], amag[:, :], cs[:, :], ALU.mult, [amag, cs], [are])
        tt(aim[:, :], amag[:, :], sn[:, :], ALU.mult, [amag, sn], [aim])
        ts(are[:, :], are[:, :], -1.0, None, ALU.add, ALU.bypass, [are], [are])
        tt(den[:, :], lre[:, :], lre[:, :], ALU.mult, [lre], [den])
        tt(tmp[:, :], lim[:, :], lim[:, :], ALU.mult, [lim], [tmp])
        tt(den[:, :], den[:, :], tmp[:, :], ALU.add, [den, tmp], [den])
        recip(den[:, :], den[:, :], [den], [den])
        tt(cre[:, :], are[:, :], lre[:, :], ALU.mult, [are, lre], [cre])
        tt(tmp[:, :], aim[:, :], lim[:, :], ALU.mult, [aim, lim], [tmp])
        tt(cre[:, :], cre[:, :], tmp[:, :], ALU.add, [cre, tmp], [cre])
        tt(cre[:, :], cre[:, :], den[:, :], ALU.mult, [cre, den], [cre])
        tt(cim[:, :], aim[:, :], lre[:, :], ALU.mult, [aim, lre], [cim])
        tt(tmp[:, :], are[:, :], lim[:, :], ALU.mult, [are, lim], [tmp])
        tt(cim[:, :], cim[:, :], tmp[:, :], ALU.subtract, [cim, tmp], [cim])
        tt(cim[:, :], cim[:, :], den[:, :], ALU.mult, [cim, den], [cim])
        cmask = cst["cmask"]
        bb3 = bbc[:, :].rearrange("p (u x) -> p u x", u=8)
        for uc in range(8):
            pb = P.ps()
            for ri, cf_ in enumerate((cre, cim)):
                for g4 in range(4):
                    gp = uc * 4 + g4
                    bc = sc()
                    ts(bc[:, 0:128], cmask[:, g4 * 128:(g4 + 1) * 128], cf_[:, gp:gp + 1], None, ALU.mult, ALU.bypass,
                       [cmask, cf_], [bc])
                    mm(pb[:, ri * 128:(ri + 1) * 128], bc[:, 0:128], ident[:, :], g4 == 0, g4 == 3, [bc, ident], [pb])
            bs = s5t()
            load(bs, bs[:, 0:256], btc[uc, :, :])
            bt = s5t()
            tt(bt[:, 0:128], bs[:, 0:128], pb[:, 0:128], ALU.mult, [bs, pb], [bt])
            tt(bt[:, 128:256], bs[:, 128:256], pb[:, 128:256], ALU.mult, [bs, pb], [bt])
            tt(bb3[:, uc, 0:128], bt[:, 0:128], bt[:, 128:256], ALU.subtract, [bt], [bbc])
            bt2 = s5t()
            tt(bt2[:, 0:128], bs[:, 0:128], pb[:, 128:256], ALU.mult, [bs, pb], [bt2])
            tt(bt2[:, 128:256], bs[:, 128:256], pb[:, 0:128], ALU.mult, [bs, pb], [bt2])
            tt(bb3[:, uc, 128:256], bt2[:, 0:128], bt2[:, 128:256], ALU.add, [bt2], [bbc])
        P.op("dve", lambda e: e.memset(s5c[:, :], 0.0), writes=[s5c])
        for h in range(4):
            ts(Sst[h][:, 0:256], cmask[:, 0:256], 0.0, None, ALU.mult, ALU.bypass, [cmask], [Sst[h]])
        for a_ in range(0, NG, 96):
            ts(alow[0:32, a_:a_ + 96], cmask[0:32, 0:96], 0.0, 1.0, ALU.mult, ALU.add, [cmask], [alow])
        tt(ac63[:, :], amag[:, :], cos3[:, :, TS - 1], ALU.mult, [amag, cosT], [ac63])
        tt(as63[:, :], amag[:, :], sin3[:, :, TS - 1], ALU.mult, [amag, sinT], [as63])

        def norm_to_hT(n, src, gcol, dsth):
            pq = P.ps()
            for c in range(NCH):
                sq = scr()
                act(sq[:, 0:n], f32(src[c][:, 0:n]), AF.Square, [src[c]], [sq])
                mm(pq[:, 0:n], ones_r[:, :], sq[:, 0:n], c == 0, c == NCH - 1, [ones_r, sq], [pq])
            ts(rstd[:, 0:n], pq[:, 0:n], 1.0 / D, EPS, ALU.mult, ALU.add, [pq], [rstd])
            act(rstd[:, 0:n], rstd[:, 0:n], AF.Ln, [rstd], [rstd])
            act(rstd[:, 0:n], rstd[:, 0:n], AF.Exp, [rstd], [rstd], scale=-0.5)
            for c in range(NCH):
                stt(dsth[c][:, 0:n], f32(src[c][:, 0:n]), vec[:, gcol + c:gcol + c + 1], rstd[:, 0:n], ALU.mult, ALU.mult,
                    [src[c], vec, rstd], [dsth[c]])

        def load_x_group(tiles, accumulate=False):
            for (dr, r0, c0, w) in tiles:
                t = kvk
                load(t, t[0:w, :], dr[r0:r0 + w, :])
                for c4 in range(4):
                    pb = P.ps()
                    for j in range(4):
                        c = c4 * 4 + j
                        tr(pb[:, j * 128:j * 128 + w], t[0:w, c * 128:(c + 1) * 128], [t], [pb])
                    for j in range(4):
                        c = c4 * 4 + j
                        if accumulate:
                            tt(xT[c][:, c0:c0 + w], f32(xT[c][:, c0:c0 + w]), pb[:, j * 128:j * 128 + w], ALU.add,
                               [xT[c], pb], [xT[c]])
                        else:
                            cp(xT[c][:, c0:c0 + w], pb[:, j * 128:j * 128 + w], [pb], [xT[c]],
                               eng=os.environ.get("KCPENG", "act") if j % 2 else "dve")

        def proj_fm(W, c0, ncols, n, src, K, consume, srct=None):
            srct = srct or src
            kc = K // 128
            Cw = (min(ncols, SLW // kc) // 128) * 128
            b = 0
            for s0 in range(0, ncols, Cw):
                cw = min(Cw, ncols - s0)
                t = wload(W, 0, kc, c0 + s0, cw)
                for bb in range(cw // 128):
                    pt = P.ps()
                    for k in range(kc):
                        mm(pt[:, 0:n], wv(t, k, cw, bb * 128, 128), src[k][:, 0:n], k == 0, k == kc - 1, [t, srct[k]], [pt])
                    consume(b, pt)
                    b += 1

        def s5_branch(n, nP, s0, ns, state_only, hook=None, drain=None):
            nsub = nP // TS
            wS = ns * 8
            PB = 2 * nP
            P.ps_list = [5, 6, 7]
            py = P.psb[4]
            RE = P.psall[:, 0:1024].rearrange("p (g c) -> p g c", g=2)
            IM = P.psall[:, 1024:2048].rearrange("p (g c) -> p g c", g=2)
            reT = [P.psb[0], P.psb[1]]
            imT = [P.psb[2], P.psb[3]]
            cnt = [0]

            def Pin(x3):
                return x3[:, :, 0:nP].rearrange("p g (s j) -> p g s j", j=TS)

            def Sin(x3):
                return x3[:, :, nP:nP + wS].rearrange("p g (s t) -> p g s t", t=8)

            def Pt(t):
                return t[:, 0:PB].rearrange("p (s g j) -> p g s j", g=2, j=TS)

            def St(t):
                return t[:, PB:PB + 2 * wS].rearrange("p (g s t) -> p g s t", g=2, t=8)

            UB = {}
            CF = {}

            def emit_bu(b, pr_):
                if pr_ == 0:
                    tw = wload(w_in, 0, 16, C_S5 + b * 128, 128)
                    pt = P.ps()
                    for kk in range(16):
                        mm(pt[:, 0:n], wv(tw, kk, 128, 0, 128), hT[kk][:, 0:n], kk == 0, kk == 15, [tw, hT[kk]], [pt])
                    u_ = s5u(); ub_ = s5ub()
                    cp(u_[:, 0:n], pt[:, 0:n], [pt], [u_], eng="act")
                    cp(ub_[:, 0:n], pt[:, 0:n], [pt], [ub_], eng="act")
                    UB[b] = (u_, ub_)
                u_, ub_ = UB[b]
                bps = []
                for i in range(2):
                    bp = bpad()
                    act(bp[:, 0:256], bb3[:, b, :], AF.Identity, [bbc, cst["rowmask"]], [bp],
                        scale=cst["rowmask"][:, pr_ * 2 + i:pr_ * 2 + i + 1])
                    bps.append(bp)
                for i in range(2):
                    mm(reT[i][:, 0:n], bps[i][:, 0:128], ub_[:, 0:n], True, True, [bps[i], ub_], [reT[i]])
                    mm(imT[i][:, 0:n], bps[i][:, 128:256], ub_[:, 0:n], True, True, [bps[i], ub_], [imT[i]])
                gq = b * 4 + pr_ * 2
                cfn = cfp()
                for i in range(2):
                    act(cfn[:, i * TS:(i + 1) * TS], cst["rmask_p"][:, 0:TS], AF.Identity, [cst["rmask_p"], amag], [cfn],
                        scale=amag[:, gq + i:gq + i + 1])
                    if ns:
                        act(cfn[:, 2 * TS + i * wS:2 * TS + (i + 1) * wS], cst["rmask_s"][:, 0:wS], AF.Identity,
                            [cst["rmask_s"], amag], [cfn], scale=amag[:, gq + i:gq + i + 1])
                CF[(b, pr_)] = cfn

            pend = []
            fin2 = []

            def flush_y():
                while fin2:
                    b_, y_, z_ = fin2.pop(0)
                    tt(ygT[b_][:, 0:n], y_[:, 0:n], z_[:, 0:n], ALU.mult, [y_, z_], [ygT_t[b_]])
                while pend:
                    b_, u_ = pend.pop(0)
                    y = sc(); z = sc()
                    stt(y[:, 0:n], u_[:, 0:n], vec[:, S5D + b_:S5D + b_ + 1], py[:, 0:n], ALU.mult, ALU.add,
                        [u_, vec, py], [y])
                    tt(z[:, 0:n], y[:, 0:n], y[:, 0:n], ALU.mult, [y], [z])
                    ts(z[:, 0:n], z[:, 0:n], 0.044715, 1.0, ALU.mult, ALU.add, [z], [z])
                    tt(z[:, 0:n], z[:, 0:n], y[:, 0:n], ALU.mult, [z, y], [z])
                    act(z[:, 0:n], z[:, 0:n], AF.Sigmoid, [z], [z], scale=1.5957691216057308)
                    fin2.append((b_, y, z))

            units = [(b_, p_) for b_ in range(8) for p_ in range(2)]
            emit_bu(*units[0])
            for ui, (b, pr_) in enumerate(units):
                if True:
                    u, ub = UB[b]
                    gp0 = b * 4 + pr_ * 2
                    cqs = []
                    if not state_only:
                        for i in range(2):
                            cq = cpad()
                            load(cq, cq[:, 0:256], ctp[gp0 + i, :, :], queue="pool")
                            cqs.append(cq)
                    k2 = 0 if state_only else cnt[0] % 2
                    cnt[0] += 1
                    ta, tb = big2[k2 * 2], big2[k2 * 2 + 1]
                    gr, gi = big2[4], big2[5]
                    cP = cos3[:, gp0:gp0 + 2, :].unsqueeze(2).to_broadcast([128, 2, nsub, TS])
                    sP = sin3[:, gp0:gp0 + 2, :].unsqueeze(2).to_broadcast([128, 2, nsub, TS])
                    regs = [(Pin, Pt, cP, sP)]
                    if ns:
                        cS = cos3[:, gp0:gp0 + 2, 0:8].unsqueeze(2).to_broadcast([128, 2, ns, 8])
                        sS = sin3[:, gp0:gp0 + 2, 0:8].unsqueeze(2).to_broadcast([128, 2, ns, 8])
                        regs.append((Sin, St, cS, sS))
                    cf = CF.pop((b, pr_))
                    for (Vi, Vt, c_, s_) in regs:
                        tt(Vt(ta), Vi(RE), c_, ALU.mult, reT + [cosT], [ta])
                        tt(Vt(tb), Vi(IM), s_, ALU.mult, imT + [sinT], [tb])
                        tt(Vt(gr), Vt(ta), Vt(tb), ALU.add, [ta, tb], [gr])
                        tt(Vt(ta), Vi(IM), c_, ALU.mult, imT + [cosT], [ta])
                        tt(Vt(tb), Vi(RE), s_, ALU.mult, reT + [sinT], [tb])
                        tt(Vt(gi), Vt(ta), Vt(tb), ALU.subtract, [ta, tb], [gi])
                    if ui + 1 < len(units):
                        emit_bu(*units[ui + 1])
                    if not state_only:
                        flush_y()
                    if hook:
                        hook()
                    t4 = sm()
                    for (gt, coff) in ((gr, 0), (gi, NGP)):
                        first = gt[:, 0:2 * TS].rearrange("p (g j) -> p g j", g=2)[:, :, 0]
                        tt(t4[:, 0:2], amag[:, gp0:gp0 + 2], s5c[:, coff + gp0:coff + gp0 + 2], ALU.mult, [amag, s5c], [t4])
                        tt(first, first, t4[:, 0:2], ALU.add, [gt, t4], [gt])
                    if ns:
                        for (gt, hoff) in ((gr, 0), (gi, 128)):
                            fs = St(gt)[:, :, :, 0]
                            h0v = s5h0[:, hoff:hoff + 128].rearrange("p (s g) -> p g s", g=NGP)[:, gp0:gp0 + 2, 0:ns]
                            tt(fs, fs, h0v, ALU.add, [gt, s5h0], [gt])
                    for s in range(nsub):
                        sl = slice(s * 2 * TS, (s + 1) * 2 * TS)
                        scan(gr[:, sl], cf[:, 0:2 * TS], gr[:, sl], 0.0, [gr, cf], [gr])
                        scan(gi[:, sl], cf[:, 0:2 * TS], gi[:, sl], 0.0, [gi, cf], [gi])
                        if s < nsub - 1:
                            lr = gr[:, sl].rearrange("p (g j) -> p g j", g=2)[:, :, TS - 1]
                            li = gi[:, sl].rearrange("p (g j) -> p g j", g=2)[:, :, TS - 1]
                            sl2 = slice((s + 1) * 2 * TS, (s + 2) * 2 * TS)
                            nr = gr[:, sl2].rearrange("p (g j) -> p g j", g=2)[:, :, 0]
                            ni = gi[:, sl2].rearrange("p (g j) -> p g j", g=2)[:, :, 0]
                            A_c = ac63[:, gp0:gp0 + 2]; A_s = as63[:, gp0:gp0 + 2]
                            t5 = sm(); t6 = sm()
                            tt(t5[:, 0:2], A_c, lr, ALU.mult, [ac63, gr], [t5])
                            tt(t6[:, 0:2], A_s, li, ALU.mult, [as63, gi], [t6])
                            tt(nr, nr, t5[:, 0:2], ALU.add, [gr, t5], [gr])
                            tt(nr, nr, t6[:, 0:2], ALU.subtract, [gr, t6], [gr])
                            t7 = sm(); t8 = sm()
                            tt(t7[:, 0:2], A_s, lr, ALU.mult, [as63, gr], [t7])
                            tt(t8[:, 0:2], A_c, li, ALU.mult, [ac63, gi], [t8])
                            tt(ni, ni, t7[:, 0:2], ALU.add, [gi, t7], [gi])
                            tt(ni, ni, t8[:, 0:2], ALU.add, [gi, t8], [gi])
                    if ns:
                        slS = slice(PB, PB + 2 * wS)
                        scan(gr[:, slS], cf[:, 2 * TS:2 * TS + 2 * wS], gr[:, slS], 0.0, [gr, cf], [gr])
                        scan(gi[:, slS], cf[:, 2 * TS:2 * TS + 2 * wS], gi[:, slS], 0.0, [gi, cf], [gi])
                    lastP = lambda t: t[:, PB - 2 * TS:PB].rearrange("p (g j) -> p g j", g=2)[:, :, TS - 1]
                    if state_only:
                        c63 = cos3[:, gp0:gp0 + 2, TS - 1]; s63 = sin3[:, gp0:gp0 + 2, TS - 1]
                        t5 = sm(); t6 = sm()
                        tt(t5[:, 0:2], c63, lastP(gr), ALU.mult, [cosT, gr], [t5])
                        tt(t6[:, 0:2], s63, lastP(gi), ALU.mult, [sinT, gi], [t6])
                        tt(s5c[:, gp0:gp0 + 2], t5[:, 0:2], t6[:, 0:2], ALU.subtract, [t5, t6], [s5c])
                        t7 = sm(); t8 = sm()
                        tt(t7[:, 0:2], s63, lastP(gr), ALU.mult, [sinT, gr], [t7])
                        tt(t8[:, 0:2], c63, lastP(gi), ALU.mult, [cosT, gi], [t8])
                        tt(s5c[:, NGP + gp0:NGP + gp0 + 2], t7[:, 0:2], t8[:, 0:2], ALU.add, [t7, t8], [s5c])
                        if hook:
                            hook()
                        continue
                    hb = scb(); hib = scb()
                    HB = hb[:, :].rearrange("p (g c) -> p g c", g=2)
                    HIB = hib[:, :].rearrange("p (g c) -> p g c", g=2)
                    for (Vi, Vt, c_, s_) in regs:
                        tt(Vt(ta), Vt(gr), c_, ALU.mult, [gr, cosT], [ta])
                        tt(Vt(tb), Vt(gi), s_, ALU.mult, [gi, sinT], [tb])
                        tt(Vt(ta), Vt(ta), Vt(tb), ALU.subtract, [ta, tb], [ta])
                    for (Vi, Vt, c_, s_) in regs:
                        act(Vi(HB), Vt(ta), AF.Identity, [ta], [hb])
                    cp(s5c[:, gp0:gp0 + 2], lastP(ta), [ta], [s5c])
                    if ns:
                        fr = s5hf[:, 0:128].rearrange("p (s g) -> p g s", g=NGP)[:, gp0:gp0 + 2, 0:ns]
                        cp(fr, St(ta)[:, :, :, 7], [ta], [s5hf])
                    for (Vi, Vt, c_, s_) in regs:
                        tt(Vt(tb), Vt(gr), s_, ALU.mult, [gr, sinT], [tb])
                        tt(Vt(gr), Vt(gi), c_, ALU.mult, [gi, cosT], [gr])
                        tt(Vt(tb), Vt(tb), Vt(gr), ALU.add, [tb, gr], [tb])
                    for (Vi, Vt, c_, s_) in regs:
                        act(Vi(HIB), Vt(tb), AF.Identity, [tb], [hib], scale=-1.0)
                    cp(s5c[:, NGP + gp0:NGP + gp0 + 2], lastP(tb), [tb], [s5c])
                    if ns:
                        fi = s5hf[:, 128:256].rearrange("p (s g) -> p g s", g=NGP)[:, gp0:gp0 + 2, 0:ns]
                        cp(fi, St(tb)[:, :, :, 7], [tb], [s5hf])
                    for i in range(2):
                        g4 = pr_ * 2 + i
                        mm(py[:, 0:n], cqs[i][:, 0:128], hb[:, i * NG:i * NG + n], g4 == 0, False, [cqs[i], hb], [py])
                        mm(py[:, 0:n], cqs[i][:, 128:256], hib[:, i * NG:i * NG + n], False, g4 == 3, [cqs[i], hib], [py])
                    if hook:
                        hook()
                if not state_only and pr_ == 1:
                    pend.append((b, u))
                    if ui == len(units) - 1:
                        flush_y()

            if not state_only:
                flush_y()
                flush_y()
            if drain:
                drain()
            P.ps_list = list(range(8))
            if state_only:
                return

            def got_glu(b, pt):
                sg = sc()
                act(sg[:, 0:n], pt[:, 0:n], AF.Sigmoid, [pt, vec], [sg], bias=vec[:, BGLU + b:BGLU + b + 1])
                tt(ybr[b][:, 0:n], f32(ygT_t[b][:, 0:n]), sg[:, 0:n], ALU.mult, [ygT_t[b], sg], [ybr_t[b]])
            proj_fm(w_glu, 0, 1024, n, ygT, 1024, got_glu, srct=ygT_t)

        def merge_gen(bi, n, first, src=None, srct=None):
            src = src or ybr
            srct = srct or ybr_t
            Cw = 256
            for s0_ in range(0, D, Cw):
                tb = wload(w_br[bi], 0, 8, s0_, Cw)
                for bb in range(Cw // 128):
                    dc = s0_ // 128 + bb
                    tg = wload(w_in, 0, 16, C_GATE + bi * D + dc * 128, 128)
                    pg = P.ps(); pp = P.ps()
                    for k in range(16):
                        mm(pg[:, 0:n], wv(tg, k, 128, 0, 128), hT[k][:, 0:n], k == 0, k == 15, [tg, hT[k]], [pg])
                    for k in range(8):
                        mm(pp[:, 0:n], wv(tb, k, Cw, bb * 128, 128), src[k][:, 0:n], k == 0, k == 7, [tb, srct[k]], [pp])
                    sg = mgs()
                    act(sg[:, 0:n], pg[:, 0:n], AF.Sigmoid, [pg], [sg])
                    yield
                    if first:
                        tt(mg[dc][:, 0:n], sg[:, 0:n], pp[:, 0:n], ALU.mult, [sg, pp], [mg[dc]])
                    else:
                        tt(sg[:, 0:n], sg[:, 0:n], pp[:, 0:n], ALU.mult, [sg, pp], [sg])
                        tt(mg[dc][:, 0:n], f32(mg[dc][:, 0:n]), sg[:, 0:n], ALU.add, [mg[dc], sg], [mg[dc]])

        def merge_branch(bi, n, first):
            for _ in merge_gen(bi, n, first):
                pass

        def gla_gen(n, tiles, s0, state_only):
            if state_only:
                g_la, g_b, g_eb, g_rm = HV(big2[2], 0), HV(big2[2], 1), HV(big2[3], 0), HV(big2[3], 1)
            else:
                g_la, g_b, g_eb, g_rm = g_la0, g_b0, g_eb0, g_rm0
            ta_ = wload(w_in, 0, 16, C_A, 16)
            pa = P.ps()
            for k in range(16):
                mm(pa[0:16, 0:n], wv(ta_, k, 16, 0, 16), hT[k][:, 0:n], k == 0, k == 15, [ta_, hT[k]], [pa])
            cp(alow[0:16, 0:n], pa[0:16, 0:n], [pa], [alow])
            for (kind, c0, w) in tiles:
                cp(g_rm[:, c0:c0 + w], cst["rmask_" + kind][:, 0:w], [cst["rmask_" + kind]], [g_rm])
            for h in range(4):
                tq = wload(w_in, 0, 16, C_Q + h * 128, 128) if not state_only else None
                tk = wload(w_in, 0, 16, C_K + h * 128, 128) if not state_only else None
                pq = P.ps(); pk = P.ps()
                if not state_only:
                    for k in range(16):
                        mm(pq[:, 0:n], wv(tq, k, 128, 0, 128), hT[k][:, 0:n], k == 0, k == 15, [tq, hT[k]], [pq])
                    for k in range(16):
                        mm(pk[:, 0:n], wv(tk, k, 128, 0, 128), hT[k][:, 0:n], k == 0, k == 15, [tk, hT[k]], [pk])
                if not state_only:
                    tk2 = tk
                    tv = wload(w_in, 0, 16, C_V + h * 256, 256)
                px = P.ps()
                mm(px[:, 0:n], wa2[0:17, h * 128:(h + 1) * 128], alow[0:17, 0:n], True, True, [wa2, alow], [px])
                act(g_la[:, 0:n], px[:, 0:n], AF.Exp, [px], [g_la], scale=-1.0)
                act(g_la[:, 0:n], g_la[:, 0:n], AF.Ln, [g_la], [g_la], bias=1.0)
                scan(g_b[:, 0:n], g_rm[:, 0:n], g_la[:, 0:n], 0.0, [g_la, g_rm], [g_b])
                act(g_eb[:, 0:n], g_b[:, 0:n], AF.Exp, [g_b], [g_eb], scale=-1.0 / 16.0)
                if not state_only:
                    act(g_enb[:, 0:n], g_b[:, 0:n], AF.Exp, [g_b], [g_enb], scale=1.0 / 16.0)
                    stt(g_qt[:, 0:n], pq[:, 0:n], 128.0 ** -0.5, g_eb[:, 0:n], ALU.mult, ALU.mult, [pq, g_eb], [g_qt_t])
                    tt(g_kt[:, 0:n], pk[:, 0:n], g_enb[:, 0:n], ALU.mult, [pk, g_enb], [g_kt_t])
                for (kind, c0, w) in tiles:
                    cs_ = slice(c0, c0 + w)
                    if state_only:
                        tk2 = wload(w_in, 0, 16, C_K + h * 128, 128)
                        tv = wload(w_in, 0, 16, C_V + h * 256, 256)
                    blk = 64 if kind == "p" else 8
                    nb = w // blk
                    pv = P.ps(); pkt = P.ps(); pxt = P.ps()
                    for k in range(16):
                        mm(pv[0:w, 0:256], hT[k][:, cs_], wv(tv, k, 256, 0, 256), k == 0, k == 15, [tv, hT[k]], [pv])
                    for k in range(16):
                        mm(pkt[0:w, 0:128], hT[k][:, cs_], wv(tk2, k, 128, 0, 128), k == 0, k == 15, [tk2, hT[k]], [pkt])
                    mm(pxt[0:w, 0:128], alow[0:17, cs_], wa2[0:17, h * 128:(h + 1) * 128], True, True, [alow, wa2], [pxt])
                    vt = tmw()
                    cp(vt[0:w, 0:256], pv[0:w, 0:256], [pv], [vt], eng="act")
                    lat = scr()
                    act(lat[0:w, 0:128], pxt[0:w, 0:128], AF.Exp, [pxt], [lat], scale=-1.0)
                    act(lat[0:w, 0:128], f32(lat[0:w, 0:128]), AF.Ln, [lat], [lat], bias=1.0)
                    prv = P.ps()
                    um = cst["umat_" + kind]
                    mm(prv[0:w, 0:128], um[0:w, 0:w], lat[0:w, 0:128], True, True, [um, lat], [prv])
                    erv = sc()
                    act(erv[0:w, 0:128], prv[0:w, 0:128], AF.Exp, [prv], [erv], scale=-1.0)
                    kh = scr()
                    if state_only:
                        pk_sb = scr()
                        cp(pk_sb[0:w, 0:128], pkt[0:w, 0:128], [pkt], [pk_sb], eng="act")
                        yield
                        tt(kh[0:w, 0:128], f32(pk_sb[0:w, 0:128]), erv[0:w, 0:128], ALU.mult, [pk_sb, erv], [kh])
                    else:
                        tt(kh[0:w, 0:128], pkt[0:w, 0:128], erv[0:w, 0:128], ALU.mult, [pkt, erv], [kh])
                    S_in = []
                    if kind == "p":
                        cur = Sst[h]
                        for bi_ in range(nb):
                            S_in.append(cur)
                            psn = P.ps()
                            r0 = bi_ * blk
                            mm(psn[:, 0:256], kh[r0:r0 + blk, 0:128], vt[r0:r0 + blk, 0:256], True, True, [kh, vt], [psn])
                            dcol = c0 + r0 + blk - 1
                            nxt = tmw()
                            stt(nxt[:, 0:256], f32(cur[:, 0:256]), g_eb[:, dcol:dcol + 1], psn[:, 0:256], ALU.mult, ALU.add,
                                [cur, g_eb, psn], [nxt])
                            cur = nxt
                        S_fin = cur
                    else:
                        for s in range(nb):
                            si = tmw()
                            load(si, si[:, 0:256], gla_s[s0 + s, h, :, :], queue="pool")
                            S_in.append(si)
                            khm = scr()
                            ts(khm[0:w, 0:128], f32(kh[0:w, 0:128]), cst["bsel_s"][0:w, s:s + 1], None, ALU.mult, ALU.bypass,
                               [kh, cst["bsel_s"]], [khm])
                            psn = P.ps()
                            mm(psn[:, 0:256], khm[0:w, 0:128], vt[0:w, 0:256], True, True, [khm, vt], [psn])
                            so = stg()
                            dcol = c0 + s * blk + blk - 1
                            stt(so[:, 0:256], f32(si[:, 0:256]), g_eb[:, dcol:dcol + 1], psn[:, 0:256], ALU.mult, ALU.add,
                                [si, g_eb, psn], [so])
                            store(so, o_glas[s0 + s, h, :, :], so[:, 0:256])
                    if not state_only:
                        pat = P.ps()
                        mm(pat[0:w, 0:w], g_kt[:, cs_], g_qt[:, cs_], True, True, [g_kt_t, g_qt_t], [pat])
                        at = scr()
                        am = cst["amask_" + kind]
                        tt(at[0:w, 0:w], pat[0:w, 0:w], am[0:w, 0:w], ALU.mult, [pat, am], [at])
                        for ec in range(2):
                            po = P.ps()
                            mm(po[:, 0:w], vt[0:w, ec * 128:(ec + 1) * 128], at[0:w, 0:w], True, False, [vt, at], [po])
                            for bi_ in range(nb):
                                c0b = bi_ * blk
                                mm(po[:, c0b:c0b + blk], S_in[bi_][:, ec * 128:(ec + 1) * 128],
                                   g_qt[:, c0 + c0b:c0 + c0b + blk], False, bi_ == nb - 1, [S_in[bi_], g_qt_t], [po])
                            cp(g_po[ec][:, cs_], po[:, 0:w], [po], [g_po[ec]], eng="act")
                    if kind == "p":
                        cp(Sst[h][:, 0:256], f32(S_fin[:, 0:256]), [S_fin], [Sst[h]])
                    yield
                if state_only:
                    continue
                pss = P.ps()
                for ec in range(2):
                    sq = scr()
                    act(sq[:, 0:n], g_po[ec][:, 0:n], AF.Square, [g_po[ec]], [sq])
                    mm(pss[:, 0:n], ones_r[:, :], sq[:, 0:n], ec == 0, ec == 1, [ones_r, sq], [pss])
                ts(g_rs[:, 0:n], pss[:, 0:n], 1.0 / 256.0, EPS, ALU.mult, ALU.add, [pss], [g_rs])
                act(g_rs[:, 0:n], g_rs[:, 0:n], AF.Ln, [g_rs], [g_rs])
                act(g_rs[:, 0:n], g_rs[:, 0:n], AF.Exp, [g_rs], [g_rs], scale=-0.5)
                trr = wload(w_in, 0, 16, C_R + h * 256, 256)
                for ec in range(2):
                    pr_ = P.ps()
                    for k in range(16):
                        mm(pr_[:, 0:n], wv(trr, k, 256, ec * 128, 128), hT[k][:, 0:n], k == 0, k == 15, [trr, hT[k]], [pr_])
                    sr = sc()
                    act(sr[:, 0:n], pr_[:, 0:n], AF.Silu, [pr_], [sr])
                    o2 = sc()
                    stt(o2[:, 0:n], g_po[ec][:, 0:n], vec[:, GLN + h * 2 + ec:GLN + h * 2 + ec + 1], g_rs[:, 0:n],
                        ALU.mult, ALU.mult, [g_po[ec], vec, g_rs], [o2])
                    tt(ybr[h * 2 + ec][:, 0:n], o2[:, 0:n], sr[:, 0:n], ALU.mult, [o2, sr], [ybr_t[h * 2 + ec]])

        def gla_branch(n, tiles, s0, state_only):
            for _ in gla_gen(n, tiles, s0, state_only):
                pass

        def attn_seq(kd, vd, qx, c0, nq, extra=()):
            kn = kvk
            load(kn, kn[:, :].rearrange("p (m c) -> p m c", m=2), kd.rearrange("(m p) c -> p m c", p=128), extra=extra)
            vn = {}
            for mc in range(2):
                for h in range(4):
                    vt_ = tmw()
                    load(vt_, vt_[:, 0:256], vd[mc * 128:(mc + 1) * 128, h * 256:(h + 1) * 256], queue="pool", extra=extra)
                    vn[(mc, h)] = vt_
            for j in range(8):
                pb = P.ps()
                for mc in range(2):
                    tr(pb[:, mc * 128:(mc + 1) * 128], kn[:, mc * 1024 + j * 128:mc * 1024 + (j + 1) * 128], [kn], [pb])
                cp(mkT[:, j * 256:(j + 1) * 256], pb[:, 0:256], [pb], [mkT], eng="act" if j % 2 else "dve")
            for h in range(4):
                pT = []
                for mc in range(2):
                    psc = P.ps()
                    for hdc in range(2):
                        j = h * 2 + hdc
                        mm(psc[:, 0:nq], mkT[:, j * 256 + mc * 128:j * 256 + (mc + 1) * 128], qx[j][:, c0:c0 + nq],
                           hdc == 0, hdc == 1, [mkT, ygT_t[j]], [psc])
                    pt_ = scr()
                    act(pt_[:, 0:nq], psc[:, 0:nq], AF.Exp, [psc], [pt_], scale=1.0 / 16.0)
                    pT.append(pt_)
                pd = P.ps()
                for mc in range(2):
                    mm(pd[:, 0:nq], ones_r[:, :], pT[mc][:, 0:nq], mc == 0, mc == 1, [ones_r, pT[mc]], [pd])
                rd = sc()
                recip(rd[:, 0:nq], pd[:, 0:nq], [pd], [rd])
                for ec in range(2):
                    po = P.ps()
                    for mc in range(2):
                        mm(po[:, 0:nq], vn[(mc, h)][:, ec * 128:(ec + 1) * 128],
                           pT[mc][:, 0:nq], mc == 0, mc == 1, [vn[(mc, h)], pT[mc]], [po])
                    tt(ybr2[h * 2 + ec][:, c0:c0 + nq], po[:, 0:nq], rd[:, 0:nq], ALU.mult, [po, rd], [ybr2[h * 2 + ec]])

        memstores = []

        def xattn_branch(n, nP, s0, ns):
            qx = ygT

            def got_q(b, pt):
                cp(qx[b][:, 0:n], pt[:, 0:n], [pt], [ygT_t[b]], eng="act" if b % 2 else "dve")
            proj_fm(w_in, C_QX, 1024, n, hT, D, got_q)
            if nP:
                attn_seq(o_mk, o_mv, qx, 0, nP, extra=memstores)
            for s in range(ns):
                attn_seq(ck[s0 + s], cv[s0 + s], qx, nP + s * 8, 8)

        def mem_kv():
            SUB = int(os.environ.get("KSUB", "9"))
            load_x_group([(memp, 0, 0, 128), (memp, 128, 128, 128)])
            if SUB < 1:
                return
            norm_to_hT(256, xT, GM, hT)
            if SUB < 2:
                return
            for (W, od) in ((w_mk, o_mk), (w_mv, o_mv)):
                for q4 in range(4):
                    t = wload(W, 0, 16, q4 * 256, 256)
                    for mt in range(2):
                        pt = P.ps()
                        for k in range(16):
                            mm(pt[:, 0:256], hT[k][:, mt * 128:(mt + 1) * 128], wv(t, k, 256, 0, 256), k == 0, k == 15,
                               [t, hT[k]], [pt])
                        so = stg()
                        cp(so[:, 0:256], pt[:, 0:256], [pt], [so])
                        memstores.append(store(so, od[mt * 128:(mt + 1) * 128, q4 * 256:(q4 + 1) * 256], so[:, 0:256]))

        def tail(n, xtiles, outs):
            for s0_ in range(0, D, 256):
                for bb in range(2):
                    dc = s0_ // 128 + bb
                    t = wload(w_out, 0, 16, dc * 128, 128)
                    pt = P.ps()
                    for k in range(16):
                        mm(pt[:, 0:n], wv(t, k, 128, 0, 128), mg[k][:, 0:n], k == 0, k == 15, [t, mg[k]], [pt])
                    cp(xT[dc][:, 0:n], pt[:, 0:n], [pt], [xT[dc]], eng="act" if bb else "dve")
            load_x_group(xtiles, accumulate=True)
            norm_to_hT(n, xT, G2, hT)
            HS = 256
            for s0_ in range(0, FFN, HS):
                acts = []
                pgs = [P.ps(), P.ps()]; pus = [P.ps(), P.ps()]
                for half in range(2):
                    tg = wload(w_fg, half * 8, 8, s0_, HS)
                    tu = wload(w_fu, half * 8, 8, s0_, HS)
                    for hc in range(HS // 128):
                        for k in range(8):
                            mm(pgs[hc][:, 0:n], wv(tg, k, HS, hc * 128, 128), hT[half * 8 + k][:, 0:n],
                               half == 0 and k == 0, half == 1 and k == 7, [tg, hT[half * 8 + k]], [pgs[hc]])
                        for k in range(8):
                            mm(pus[hc][:, 0:n], wv(tu, k, HS, hc * 128, 128), hT[half * 8 + k][:, 0:n],
                               half == 0 and k == 0, half == 1 and k == 7, [tu, hT[half * 8 + k]], [pus[hc]])
                for hc in range(HS // 128):
                    sg = sc()
                    act(sg[:, 0:n], pgs[hc][:, 0:n], AF.Silu, [pgs[hc]], [sg])
                    a_ = scr()
                    tt(a_[:, 0:n], sg[:, 0:n], pus[hc][:, 0:n], ALU.mult, [sg, pus[hc]], [a_])
                    acts.append(a_)
                tds = [wload(w_fd, s0_ // 128 + hc, 1, 0, D) for hc in range(HS // 128)]
                for dc in range(NCH):
                    pd = P.ps()
                    for hc in range(HS // 128):
                        mm(pd[:, 0:n], wv(tds[hc], 0, D, dc * 128, 128), acts[hc][:, 0:n], hc == 0, hc == HS // 128 - 1,
                           [tds[hc], acts[hc]], [pd])
                    tt(xT[dc][:, 0:n], f32(xT[dc][:, 0:n]), pd[:, 0:n], ALU.add, [xT[dc], pd], [xT[dc]])
            pq = P.ps()
            for c in range(NCH):
                sq = scr()
                act(sq[:, 0:n], f32(xT[c][:, 0:n]), AF.Square, [xT[c]], [sq])
                mm(pq[:, 0:n], ones_r[:, :], sq[:, 0:n], c == 0, c == NCH - 1, [ones_r, sq], [pq])
            ts(rstd[:, 0:n], pq[:, 0:n], 1.0 / D, EPS, ALU.mult, ALU.add, [pq], [rstd])
            act(rstd[:, 0:n], rstd[:, 0:n], AF.Ln, [rstd], [rstd])
            act(rstd[:, 0:n], rstd[:, 0:n], AF.Exp, [rstd], [rstd], scale=-0.5)
            for c in range(NCH):
                stt(xT[c][:, 0:n], f32(xT[c][:, 0:n]), vec[:, GF + c:GF + c + 1], rstd[:, 0:n], ALU.mult, ALU.mult,
                    [xT[c], vec, rstd], [xT[c]])
            for (od, r0, c0, w) in outs:
                for q4 in range(4):
                    pb = P.ps()
                    for j in range(4):
                        c = q4 * 4 + j
                        tr(pb[0:w, j * 128:(j + 1) * 128], f32(xT[c][:, c0:c0 + w]), [xT[c]], [pb])
                    ot = stg()
                    cp(ot[0:w, 0:512], pb[0:w, 0:512], [pb], [ot], eng="act" if q4 % 2 else "dve")
                    store(ot, od[r0:r0 + w, q4 * 512:(q4 + 1) * 512], ot[0:w, 0:512])

        STAGE = int(os.environ.get("KSTAGE", "99"))
        if STAGE >= 1:
            mem_kv()
        for g in range(4 if STAGE >= 2 else 0):
            xt = [(xpre, g * 256, 0, 128), (xpre, g * 256 + 128, 128, 128)]
            load_x_group(xt)
            norm_to_hT(256, xT, G1, hT)
            gg = gla_gen(256, [("p", 0, 128), ("p", 128, 128)], 0, True)

            def drain_g(gg=gg):
                for _ in gg:
                    pass
            s5_branch(256, 256, 0, 0, True, hook=lambda gg=gg: next(gg, None), drain=drain_g)
        NGRP = int(os.environ.get("KGROUPS", "4"))
        for g in range(NGRP if STAGE >= 3 else 0):
            n = NG
            xt = [(xp, g * 256, 0, 128), (xp, g * 256 + 128, 128, 128), (xs, g * 32, 256, 32)]
            outs = [(o_yp, g * 256, 0, 128), (o_yp, g * 256 + 128, 128, 128), (o_ys, g * 32, 256, 32)]
            load_x_group(xt)
            norm_to_hT(n, xT, G1, hT)
            gla_branch(n, [("p", 0, 128), ("p", 128, 128), ("s", 256, 32)], g * 4, False)
            xattn_branch(n, 256, g * 4, 4)
            for ri, src_ in enumerate((s5re_s, s5im_s)):
                st_ = stg()
                load(st_, st_[:, 0:128], src_[g * 128:(g + 1) * 128, :])
                pb = P.ps()
                tr(pb[:, 0:128], st_[:, 0:128], [st_], [pb])
                tt(s5h0[:, ri * 128:(ri + 1) * 128].rearrange("p (s g) -> p s g", g=NGP),
                   pb[:, 0:128].rearrange("p (s g) -> p s g", g=NGP),
                   amag[:, :].unsqueeze(1).to_broadcast([128, 4, NGP]), ALU.mult, [pb, amag], [s5h0])

            def _chain(n=n):
                for _ in merge_gen(1, n, True):
                    yield
                for _ in merge_gen(2, n, False, src=ybr2, srct=ybr2):
                    yield
            mgn = _chain()

            def drain_m(mgn=mgn):
                for _ in mgn:
                    pass
            s5_branch(n, 256, g * 4, 4, False, hook=lambda mgn=mgn: next(mgn, None), drain=drain_m)
            for ri in range(2):
                pb = P.ps()
                tr(pb[:, 0:128], s5hf[:, ri * 128:(ri + 1) * 128], [s5hf], [pb])
                so = stg()
                cp(so[:, 0:128], pb[:, 0:128], [pb], [so])
                store(so, o_s5s[ri, g * 128:(g + 1) * 128, :], so[:, 0:128])
            merge_branch(0, n, False)
            if STAGE >= 7:
                tail(n, xt, outs)
        for ri in range(2):
            pb = P.ps()
            tr(pb[0:NGP, 0:128], s5c[:, ri * NGP:(ri + 1) * NGP], [s5c], [pb])
            so = stg()
            cp(so[0:NGP, 0:128], pb[0:NGP, 0:128], [pb], [so])
            store(so, o_s5p[ri, :, :], so[0:NGP, 0:128])
        for h in range(4):
            so = stg()
            cp(so[:, 0:256], f32(Sst[h][:, 0:256]), [Sst[h]], [so])
            store(so, o_glap[h, :, :], so[:, 0:256])

        P.emit()
    return nc


_NC_CACHE = {}


def _host_layouts(inp):
    f = np.float32
    shared = {}
    shared["w_in"] = np.ascontiguousarray(inp["w_in"][0])
    shared["w_glu"] = np.ascontiguousarray(inp["s5_w_glu"][0])
    shared["w_mk"] = np.ascontiguousarray(inp["w_mem_k"][0])
    shared["w_mv"] = np.ascontiguousarray(inp["w_mem_v"][0])
    shared["w_br0"] = np.ascontiguousarray(inp["w_br_s5"][0])
    shared["w_br1"] = np.ascontiguousarray(inp["w_br_gla"][0])
    shared["w_br2"] = np.ascontiguousarray(inp["w_br_xattn"][0])
    shared["w_out"] = np.ascontiguousarray(inp["w_out"][0])
    shared["w_fg"] = np.ascontiguousarray(inp["w_ffn_gate"][0])
    shared["w_fu"] = np.ascontiguousarray(inp["w_ffn_up"][0])
    shared["w_fd"] = np.ascontiguousarray(inp["w_ffn_down"][0])
    shared["w_a2aug"] = np.ascontiguousarray(np.concatenate([inp["gla_w_a2"][0], inp["gla_b_a"][0][None, :]], 0)).astype(f)
    vecs = np.zeros((128, 96), f)

    def colz(v):
        return np.ascontiguousarray(v.reshape(-1, 128).T)
    vecs[:, 0:16] = colz(inp["norm_mix"][0]); vecs[:, 16:32] = colz(inp["norm_ffn"][0])
    vecs[:, 32:48] = colz(inp["norm_final"]); vecs[:, 48:64] = colz(inp["mem_norm"][0])
    vecs[:, 64:72] = colz(inp["gla_norm"][0]); vecs[:, 72:80] = colz(inp["s5_d"][0].reshape(-1))
    vecs[:, 80:88] = colz(inp["s5_b_glu"][0])
    shared["vecs"] = vecs

    def qlay(a):
        return np.ascontiguousarray(a.reshape(NGP, 2, PST).transpose(1, 2, 0).reshape(128, NGP))
    shared["lamre"] = qlay(inp["s5_lam_re"][0]); shared["lamim"] = qlay(inp["s5_lam_im"][0])
    shared["logdt"] = qlay(np.broadcast_to(inp["s5_log_dt"][0][:, None], (G, PST)))
    btc = np.zeros((8, 128, 256), f)
    for ri, src in enumerate((inp["s5_b_re"][0], inp["s5_b_im"][0])):
        for gp in range(NGP):
            uc, g4 = gp // 4, gp % 4
            for two in range(2):
                btc[uc, g4 * 32 + two * 16:g4 * 32 + two * 16 + 16, ri * 128 + two * 64:ri * 128 + (two + 1) * 64] = src[2 * gp + two].T
    shared["btc"] = btc
    ctp = np.zeros((NGP, 128, 256), f)
    for ri, src in enumerate((inp["s5_c_re"][0], inp["s5_c_im"][0])):
        for gp in range(NGP):
            g4 = gp % 4
            for two in range(2):
                ctp[gp, two * 64:(two + 1) * 64, ri * 128 + g4 * 32 + two * 16:ri * 128 + g4 * 32 + two * 16 + 16] = src[2 * gp + two].T
    shared["ctp"] = ctp
    for k, v in _consts().items():
        shared["c_" + k] = v
    return shared


def kernel(**inp):
    inp = {k: np.asarray(v) for k, v in inp.items()}
    if "nc" not in _NC_CACHE:
        _NC_CACHE["nc"] = build_program()
    nc = _NC_CACHE["nc"]
    shared = _host_layouts(inp)
    xpr = inp["x_prompt"]; xsm = inp["x_sample"]
    in_maps = []
    for c in range(8):
        b, half = c // 2, c % 2
        m = dict(shared)
        m["xp"] = np.ascontiguousarray(xpr[b, half * 1024:(half + 1) * 1024])
        m["xpre"] = np.ascontiguousarray(xpr[b, 0:1024]) if half == 1 else np.zeros((1024, D), np.float32)
        sq = slice(16 * c, 16 * c + 16)
        m["xs"] = np.ascontiguousarray(xsm[sq].reshape(128, D))
        m["memp"] = np.ascontiguousarray(inp["mem_prompt"][b])
        m["s5re_s"] = np.ascontiguousarray(inp["state_s5_re"][0, sq].reshape(512, 128))
        m["s5im_s"] = np.ascontiguousarray(inp["state_s5_im"][0, sq].reshape(512, 128))
        m["gla_s"] = np.ascontiguousarray(inp["state_gla"][0, sq])
        m["ck"] = np.ascontiguousarray(inp["cache_mem_k"][0, sq].reshape(16, 256, 1024))
        m["cv"] = np.ascontiguousarray(inp["cache_mem_v"][0, sq].reshape(16, 256, 1024))
        in_maps.append(m)
    NCORE = int(os.environ.get("KCORES", "8"))
    res = run_bass_kernel_spmd(nc, in_maps[:NCORE], core_ids=list(range(NCORE)))
    R = list(res.results) + [None] * (8 - NCORE)
    f = np.float32
    y_prompt = np.zeros((4, 2048, D), f); y_sample = np.zeros((128, 8, D), f)
    s5p_re = np.zeros((1, 4, G, PST), f); s5p_im = np.zeros((1, 4, G, PST), f)
    glap = np.zeros((1, 4, 4, 128, 256), f)
    mk = np.zeros((1, 4, 256, 4, 256), f); mv = np.zeros((1, 4, 256, 4, 256), f)
    s5s_re = np.zeros((1, 128, G, PST), f); s5s_im = np.zeros((1, 128, G, PST), f)
    glas = np.zeros((1, 128, 4, 128, 256), f)
    for c in range(8):
        b, half = c // 2, c % 2
        r = R[c]
        if r is None:
            continue
        y_prompt[b, half * 1024:(half + 1) * 1024] = r["o_yp"]
        sq = slice(16 * c, 16 * c + 16)
        y_sample[sq] = r["o_ys"].reshape(16, 8, D)
        s5s_re[0, sq] = r["o_s5s"][0].reshape(16, G, PST)
        s5s_im[0, sq] = r["o_s5s"][1].reshape(16, G, PST)
        glas[0, sq] = r["o_glas"]
        if half == 1:
            s5p_re[0, b] = r["o_s5p"][0].reshape(G, PST)
            s5p_im[0, b] = r["o_s5p"][1].reshape(G, PST)
            glap[0, b] = r["o_glap"]
        else:
            mk[0, b] = r["o_mk"].reshape(256, 4, 256)
            mv[0, b] = r["o_mv"].reshape(256, 4, 256)
    return (y_prompt, y_sample, s5p_re, s5p_im, glap, mk, mv, s5s_re, s5s_im, glas)
```
